# Optimizing a Trainium2 kernel written in Bass

```python
import jax, jax.numpy as jnp
from jax import lax
import numpy as np

D_MODEL = 1024
BATCH = 8
SEQ = 8192
DEPTH = 4

N_META = 16
LRU_WIDTH = D_MODEL
LRU_HEADS = 8
LRU_HEAD_DIM = LRU_WIDTH // LRU_HEADS
LRU_CONV = 4
LRU_C = 8.0
CONV_WIDTH = D_MODEL // 2
CONV_GROUPS = 4
CONV_KERNEL = 31
MIX_WIDTH = LRU_WIDTH + CONV_WIDTH
IN_WIDTH = 2 * LRU_WIDTH + 2 * CONV_WIDTH
D_FF = 3 * D_MODEL
FFN_CONV = 3
EPS = 1e-6

kernel_name = 'hybrid_rglru_conformer_conv_trunk'


def rms_norm(x, g):
    xf = x.astype(jnp.float32)
    y = xf * lax.rsqrt(jnp.mean(xf * xf, axis=-1, keepdims=True) + EPS)
    return (y * g.astype(jnp.float32)).astype(x.dtype)


def group_layer_norm(x, g, b, groups):
    shp = x.shape
    xf = x.astype(jnp.float32).reshape(shp[:-1] + (groups, shp[-1] // groups))
    mu = jnp.mean(xf, axis=-1, keepdims=True)
    var = jnp.mean(jnp.square(xf - mu), axis=-1, keepdims=True)
    y = ((xf - mu) * lax.rsqrt(var + EPS)).reshape(shp)
    return (y * g.astype(jnp.float32) + b.astype(jnp.float32)).astype(x.dtype)


def causal_dwconv(x, w, b):
    K, C = w.shape
    y = lax.conv_general_dilated(
        x, w[:, None, :].astype(x.dtype), window_strides=(1,),
        padding=[(K - 1, 0)], dimension_numbers=('NWC', 'WIO', 'NWC'),
        feature_group_count=C)
    return y + b.astype(x.dtype)


def rg_lru(x, w_a, b_a, w_x, b_x, lam):
    B, S, W = x.shape
    xh = x.reshape(B, S, LRU_HEADS, LRU_HEAD_DIM)
    r = jax.nn.sigmoid(jnp.einsum('bshi,hij->bshj', xh, w_a).reshape(B, S, W) + b_a)
    i = jax.nn.sigmoid(jnp.einsum('bshi,hij->bshj', xh, w_x).reshape(B, S, W) + b_x)
    log_a = -LRU_C * r.astype(jnp.float32) * jax.nn.softplus(-lam.astype(jnp.float32))
    a = jnp.exp(log_a)
    mult = jnp.sqrt(-jnp.expm1(2.0 * log_a))
    u = mult * (i * x).astype(jnp.float32)

    def combine(left, right):
        a_l, b_l = left
        a_r, b_r = right
        return a_l * a_r, a_r * b_l + b_r

    _, h = lax.associative_scan(combine, (a, u), axis=1)
    return h.astype(x.dtype)


def hybrid_layer(h, g_pre_mix, w_in, lru_conv_w, lru_conv_b, lru_wa, lru_ba, lru_wx, lru_bx,
                 lru_lambda, conv_w, conv_b, conv_ln_g, conv_ln_b, g_out_lru, g_out_conv,
                 w_out, g_post_mix, g_pre_ffn, w_up, ffn_conv_w, ffn_conv_b, w_down, g_post_ffn):
    z = rms_norm(h, g_pre_mix)
    proj = jnp.einsum('bsd,de->bse', z, w_in)
    x_lru, g_lru, c_a, c_b = jnp.split(
        proj, [LRU_WIDTH, 2 * LRU_WIDTH, 2 * LRU_WIDTH + CONV_WIDTH], axis=-1)
    x_lru = causal_dwconv(x_lru, lru_conv_w, lru_conv_b)
    y_a = rg_lru(x_lru, lru_wa, lru_ba, lru_wx, lru_bx, lru_lambda) * jax.nn.gelu(g_lru)
    c = c_a * jax.nn.sigmoid(c_b)
    c = causal_dwconv(c, conv_w, conv_b)
    y_b = jax.nn.silu(group_layer_norm(c, conv_ln_g, conv_ln_b, CONV_GROUPS))
    y = jnp.concatenate([rms_norm(y_a, g_out_lru), rms_norm(y_b, g_out_conv)], axis=-1)
    h = h + rms_norm(jnp.einsum('bse,ed->bsd', y, w_out), g_post_mix)
    z = rms_norm(h, g_pre_ffn)
    u = causal_dwconv(jnp.einsum('bsd,df->bsf', z, w_up), ffn_conv_w, ffn_conv_b)
    gate, up = jnp.split(u, 2, axis=-1)
    f = jnp.einsum('bsf,fd->bsd', jax.nn.gelu(gate) * up, w_down)
    return h + rms_norm(f, g_post_ffn)


def _normal(k, shape, scale):
    return jax.random.normal(k, shape, jnp.float32) * scale


def setup_inputs(seed: int = 0) -> dict:
    key = jax.random.key(seed)
    ks = jax.random.split(key, 32)
    L, D = DEPTH, D_MODEL

    def gain(k, n):
        return 1.0 + _normal(k, (L, n), 0.02)

    u = jax.random.uniform(ks[10], (L, LRU_WIDTH), jnp.float32, minval=0.9, maxval=0.999)
    a_base = u ** (1.0 / LRU_C)
    lam = jnp.log(a_base) - jnp.log1p(-a_base)
    return {
        'x': _normal(ks[0], (BATCH, SEQ, D), 1.0),
        'meta_tokens': _normal(ks[1], (N_META, D), 1.0),
        'g_pre_mix': gain(ks[2], D),
        'w_in': _normal(ks[3], (L, D, IN_WIDTH), D ** -0.5),
        'lru_conv_w': _normal(ks[4], (L, LRU_CONV, LRU_WIDTH), LRU_CONV ** -0.5),
        'lru_conv_b': _normal(ks[5], (L, LRU_WIDTH), 0.01),
        'lru_wa': _normal(ks[6], (L, LRU_HEADS, LRU_HEAD_DIM, LRU_HEAD_DIM), LRU_HEAD_DIM ** -0.5),
        'lru_ba': _normal(ks[7], (L, LRU_WIDTH), 0.01),
        'lru_wx': _normal(ks[8], (L, LRU_HEADS, LRU_HEAD_DIM, LRU_HEAD_DIM), LRU_HEAD_DIM ** -0.5),
        'lru_bx': _normal(ks[9], (L, LRU_WIDTH), 0.01),
        'lru_lambda': lam,
        'conv_w': _normal(ks[11], (L, CONV_KERNEL, CONV_WIDTH), CONV_KERNEL ** -0.5),
        'conv_b': _normal(ks[12], (L, CONV_WIDTH), 0.01),
        'conv_ln_g': gain(ks[13], CONV_WIDTH),
        'conv_ln_b': _normal(ks[14], (L, CONV_WIDTH), 0.01),
        'g_out_lru': gain(ks[15], LRU_WIDTH),
        'g_out_conv': gain(ks[16], CONV_WIDTH),
        'w_out': _normal(ks[17], (L, MIX_WIDTH, D), MIX_WIDTH ** -0.5),
        'g_post_mix': gain(ks[18], D),
        'g_pre_ffn': gain(ks[19], D),
        'w_up': _normal(ks[20], (L, D, 2 * D_FF), D ** -0.5),
        'ffn_conv_w': _normal(ks[21], (L, FFN_CONV, 2 * D_FF), FFN_CONV ** -0.5),
        'ffn_conv_b': _normal(ks[22], (L, 2 * D_FF), 0.01),
        'w_down': _normal(ks[23], (L, D_FF, D), D_FF ** -0.5),
        'g_post_ffn': gain(ks[24], D),
    }


def reference(x, meta_tokens, g_pre_mix, w_in, lru_conv_w, lru_conv_b, lru_wa, lru_ba, lru_wx,
              lru_bx, lru_lambda, conv_w, conv_b, conv_ln_g, conv_ln_b, g_out_lru, g_out_conv,
              w_out, g_post_mix, g_pre_ffn, w_up, ffn_conv_w, ffn_conv_b, w_down, g_post_ffn):
    B = x.shape[0]
    meta = jnp.broadcast_to(meta_tokens[None].astype(x.dtype), (B, N_META, D_MODEL))
    h = jnp.concatenate([meta, x], axis=1)
    for l in range(DEPTH):
        h = hybrid_layer(
            h, g_pre_mix[l], w_in[l], lru_conv_w[l], lru_conv_b[l], lru_wa[l], lru_ba[l],
            lru_wx[l], lru_bx[l], lru_lambda[l], conv_w[l], conv_b[l], conv_ln_g[l],
            conv_ln_b[l], g_out_lru[l], g_out_conv[l], w_out[l], g_post_mix[l],
            g_pre_ffn[l], w_up[l], ffn_conv_w[l], ffn_conv_b[l], w_down[l], g_post_ffn[l])
    return h[:, N_META:, :]
```

```python
import numpy as np
import concourse.bass as bass
import concourse.mybir as mybir
from concourse.bass_utils import run_bass_kernel_spmd

F32 = mybir.dt.float32
BF16 = mybir.dt.bfloat16
AF = mybir.ActivationFunctionType
ALU = mybir.AluOpType

D = 1024
SEQ = 8192
NMETA = 16
STOT = SEQ + NMETA
T = 456
NT_FULL = STOT // T
DEPTH = 4
NCH = 8
EPS = 1e-6
NS = 4
import os as _os
FFN_PIPE = int(_os.environ.get('FFN_PIPE', '0'))
SLOT = 4096

OFF_IN, OFF_G, OFF_OUT, OFF_UP, OFF_DOWN = 0, 24576, 26624, 38912, 88064
WCOLS = 112640
IN_ORDER = list(range(0, 8)) + [20, 16, 21, 17, 22, 18, 23, 19] + list(range(8, 16))
PIECES = ([("in", q, OFF_IN + q * 4096, 4096) for q in range(2)]
          + [("g", 0, OFF_G, 2048)]
          + [("in", q, OFF_IN + q * 4096, 4096) for q in (4, 5, 2, 3)]
          + [("out", q, OFF_OUT + q * 3072, 3072) for q in range(4)]
          + [("up", q, OFF_UP + q * 4096, 4096) for q in range(12)]
          + [("down", q, OFF_DOWN + q * 3072, 3072) for q in range(8)])
NPIECE = len(PIECES)

P_GPRE, P_LCW, P_LCB, P_BA, P_BX, P_LAM, P_CW, P_CB, P_LNG, P_LNB = 0, 8, 40, 48, 56, 64, 72, 196, 200, 204
P_GOL, P_GOC, P_GPM, P_GPF, P_FCW, P_FCB, P_GPO = 208, 216, 220, 228, 236, 380, 428
NP = 436

ENGS = ("pe", "act", "dve", "pool", "sp")


class Sched:
    def __init__(self, nc, needed=None):
        self.nc = nc
        self.needed = needed
        self.targets = {e: set() for e in ENGS}
        self.streams = {e: [] for e in ENGS}
        self.count = {e: 0 for e in ENGS}
        self.waited = {e: {} for e in ENGS}
        self.res = {}
        self.dma_count = {}
        self.sem_names = set(ENGS)
        self.dummy = {}
        self.nwaits = 0
        self.ndummy = 0
        self.nops = 0

    def _deps(self, reads, writes):
        deps = {}
        for k in reads:
            r = self.res.get(k)
            if r is not None and r[0] is not None:
                s, v = r[0]
                if deps.get(s, -1) < v:
                    deps[s] = v
        for k in writes:
            r = self.res.get(k)
            if r is not None:
                if r[0] is not None:
                    s, v = r[0]
                    if deps.get(s, -1) < v:
                        deps[s] = v
                for s, v in r[1].items():
                    if deps.get(s, -1) < v:
                        deps[s] = v
        return deps

    def _commit(self, reads, writes, tok):
        s, v = tok
        for k in reads:
            r = self.res.get(k)
            if r is None:
                r = self.res[k] = [None, {}]
            if r[1].get(s, -1) < v:
                r[1][s] = v
        for k in writes:
            self.res[k] = [tok, {}]

    def _mkwaits(self, eng, deps):
        w = []
        wd = self.waited[eng]
        for s, v in deps.items():
            if wd.get(s, -1) >= v:
                continue
            wd[s] = v
            w.append((s, v))
            if s in self.targets:
                self.targets[s].add(v)
        self.nwaits += len(w)
        return w

    def op(self, eng, fn, reads=(), writes=()):
        deps = self._deps(reads, writes)
        if eng == "pe":
            deps.pop("pe", None)
        st = self.streams[eng]
        if (deps.get(eng, -1) == self.count[eng] and self.waited[eng].get(eng, -1) < self.count[eng]
                and eng in self.dummy and st and st[-1][2] == (eng, 1)):
            st.append(([], self.dummy[eng], None, 0))
            self.ndummy += 1
        waits = self._mkwaits(eng, deps)
        self.count[eng] += 1
        tok = (eng, self.count[eng])
        st.append((waits, fn, (eng, 1), self.count[eng]))
        self._commit(reads, writes, tok)
        self.nops += 1
        return tok

    def dma(self, eng, chan, fn, reads=(), writes=()):
        writes = list(writes) + [("chan", chan)]
        deps = self._deps(reads, writes)
        waits = self._mkwaits(eng, deps)
        n = self.dma_count.get(chan, 0) + 1
        self.dma_count[chan] = n
        self.sem_names.add(chan)
        tok = (chan, 16 * n)
        self.streams[eng].append((waits, fn, (chan, 16), 0))
        self._commit(reads, writes, tok)
        return tok

    def alias(self, new_keys, old_keys):
        merged = {}
        for k in old_keys:
            r = self.res.get(k)
            if r is None:
                continue
            if r[0] is not None:
                s, v = r[0]
                if merged.get(s, -1) < v:
                    merged[s] = v
            for s, v in r[1].items():
                if merged.get(s, -1) < v:
                    merged[s] = v
        for k in new_keys:
            self.res[k] = [None, dict(merged)]

    def final_wait(self, eng, toks):
        deps = {}
        for s, v in toks:
            if deps.get(s, -1) < v:
                deps[s] = v
        self.streams[eng].append((self._mkwaits(eng, deps), None, None, 0))

    def emit(self):
        nc = self.nc
        sems = {n: nc.alloc_semaphore("s_" + n) for n in sorted(self.sem_names)}
        streams = self.streams

        needed = self.needed
        rank = {}
        if needed is not None:
            for e in ENGS:
                rank[e] = {v: i + 1 for i, v in enumerate(sorted(needed[e]))}

        def run(engobj, name):
            for waits, fn, inc, idx in streams[name]:
                for s, v in waits:
                    if needed is not None and s in rank:
                        v = rank[s][v]
                    engobj.wait_ge(sems[s], v)
                if fn is not None:
                    ins = fn(engobj)
                    if inc is not None:
                        if needed is not None and inc[0] in rank and idx not in rank[inc[0]]:
                            continue
                        ins.then_inc(sems[inc[0]], inc[1])

        with nc.Block() as block:
            @block.tensor
            def _(e):
                run(e, "pe")

            @block.scalar
            def _(e):
                run(e, "act")

            @block.vector
            def _(e):
                run(e, "dve")

            @block.gpsimd
            def _(e):
                run(e, "pool")

            @block.sync
            def _(e):
                run(e, "sp")


def build_nc(NT=NT_FULL, L=DEPTH, dbg=None):
    _, S1 = _build(NT, L, None, emit=False)
    return _build(NT, L, S1.targets, emit=True)


def _build(NT, L, needed, emit):
    nc = bass.Bass("TRN2", target_bir_lowering=False)
    ntok = NT * T
    hT = nc.dram_tensor("hT", [NCH, 128, ntok], F32, kind="ExternalInput").ap()
    wts = nc.dram_tensor("wts", [L, 128, WCOLS], F32, kind="ExternalInput").ap()
    prm_d = nc.dram_tensor("prm", [128, L * NP], F32, kind="ExternalInput").ap()
    ident_d = nc.dram_tensor("ident", [128, 128], F32, kind="ExternalInput").ap()
    cen_d = nc.dram_tensor("cen", [128, 128], F32, kind="ExternalInput").ap()
    oT = nc.dram_tensor("oT", [NCH, 128, ntok - NMETA], F32, kind="ExternalOutput").ap()
    wbf = nc.dram_tensor("wbf", [L, 128, WCOLS], BF16, kind="Internal").ap()

    S = Sched(nc, needed)
    A = nc.alloc_sbuf_tensor

    prm = A("prm_sb", [128, L * NP], F32)
    cst = A("cst", [128, 8 + 2 * L * NCH], F32)
    ones_bf = A("ones_bf", [128, 128], BF16)
    ones5_bf = A("ones5_bf", [128, 128], BF16)
    ones_f = A("ones_f", [128, 128], F32)
    XY = [A("hx", [128, NCH * T], F32), A("hy", [128, NCH * T], F32)]
    z = A("z", [128, NCH * T], BF16)
    sq = A("sq", [128, NCH * T], BF16)
    rstd = [A(f"rstd{i}", [128, T], F32) for i in range(2)]
    o = A("o", [128, NCH * T], F32)
    XLW = 3 + T
    CBW = 30 + T
    ar = A("arena", [128, 5604], F32)
    xl = ar[:, 0:1836].bitcast(BF16)
    gg = ar[:, 1836:3660].bitcast(BF16)
    cbuf = ar[:, 3660:4632].bitcast(BF16)
    affn = ar[:, 0:5472].bitcast(BF16)
    cc = A("cc", [128, 4 * T], F32)
    Mb = A("Mb", [128, 4 * T], F32)
    xcf = A("xcf", [128, 4 * T], F32)
    xcb = A("xcb", [128, 4 * T], BF16)
    hl = [A(f"hl{i}", [128, T], F32) for i in range(2)]
    y = A("y", [128, 12 * T], BF16)
    sig = [A(f"sig{i}", [128, T], F32) for i in range(2)]
    cen_f = A("cen_f", [128, 128], F32)
    ident_bf = A("ident_bf", [128, 128], BF16)
    dg31 = [A(f"dg31_{i}", [128, 2048], BF16) for i in range(3)]
    dg4 = A("dg4", [128, 2048], BF16)
    YBW = 2 + T
    ybuf = [A(f"ybuf{i}", [128, 2 * YBW], BF16) for i in range(2)]
    tbuf = [A(f"tbuf{i}", [128, T], F32) for i in range(4)]
    gl = [A(f"gl{i}", [128, T], BF16) for i in range(2)]
    tailx = A("tailx", [128, L * NCH * 3], BF16)
    tailc = A("tailc", [128, L * 4 * 30], BF16)
    taily = A("taily", [128, L * 48 * 2], BF16)
    state = A("state", [128, L * NCH], F32)
    ring = [A(f"ring{i}", [128, SLOT], BF16) for i in range(NS)]
    wgbuf = A("wgbuf", [128, 2048], BF16)
    ps = [nc.alloc_psum_tensor(f"ps{i}", [128, 512], F32) for i in range(8)]

    S.dummy["act"] = lambda e: e.activation(out=cst[:, 2:3], in_=cst[:, 3:4], func=AF.Copy)
    S.dummy["dve"] = lambda e: e.tensor_copy(out=cst[:, 4:5], in_=cst[:, 5:6])
    S.dummy["pool"] = lambda e: e.tensor_copy(out=cst[:, 6:7], in_=cst[:, 7:8])

    def v3(ap2, c):
        return ap2.rearrange("p (c t) -> p c t", c=c)

    def col(l, off, i=0):
        b = l * NP + off + i
        return prm[:, b:b + 1]

    bank_ctr = [0]

    def next_bank():
        b = bank_ctr[0] % 8
        bank_ctr[0] += 1
        return b

    S.dma("sp", "ldp", lambda e: e.dma_start(out=prm[:], in_=prm_d), writes=["prm"])
    S.dma("pool", "ldi", lambda e: e.dma_start(out=ident_bf[:], in_=ident_d), writes=["ident"])
    S.dma("sp", "ldc", lambda e: e.dma_start(out=cen_f[:], in_=cen_d), writes=["cen_f"])
    S.op("pool", lambda e: e.memset(cst[:, 0:1], EPS), writes=["cst"])
    S.op("pool", lambda e: e.memset(cst[:, 1:2], 1.0), writes=["cst1"])
    S.op("pool", lambda e: e.memset(cst[:, 2:8], 0.0), writes=["cstd"])
    S.op("pool", lambda e: e.memset(ones_bf[:], 1.0 / 1024.0), writes=["ones_bf"])
    S.op("pool", lambda e: e.memset(ones5_bf[:], 1.0 / 512.0), writes=["ones5_bf"])
    S.op("pool", lambda e: e.memset(ones_f[:], 1.0 / 128.0), writes=["ones_f"])
    S.op("pool", lambda e: e.memset(tailx[:], 0.0), writes=[("tailx", l) for l in range(L)])
    S.op("pool", lambda e: e.memset(tailc[:], 0.0), writes=[("tailc", l) for l in range(L)])
    S.op("pool", lambda e: e.memset(taily[:], 0.0), writes=[("taily", l, j) for l in range(L) for j in range(24)])
    S.op("pool", lambda e: e.memset(state[:], 0.0), writes=[("st", l, m) for l in range(L) for m in range(NCH)])
    eps_c = cst[:, 0:1]
    one_c = cst[:, 1:2]
    C1 = 8
    C2 = 8 + L * NCH
    for l in range(L):
        c1s = cst[:, C1 + l * NCH:C1 + (l + 1) * NCH]
        c2s = cst[:, C2 + l * NCH:C2 + (l + 1) * NCH]
        lam = prm[:, l * NP + P_LAM:l * NP + P_LAM + NCH]
        S.op("act", lambda e, c1s=c1s, lam=lam: e.activation(out=c1s, in_=lam, func=AF.Exp, scale=-1.0),
             reads=["prm"], writes=[("c1", l)])
        S.op("act", lambda e, c1s=c1s: e.activation(out=c1s, in_=c1s, func=AF.Ln, bias=one_c, scale=1.0),
             reads=["cst1"], writes=[("c1", l)])
        S.op("dve", lambda e, c1s=c1s, c2s=c2s: e.tensor_scalar(out=c2s, in0=c1s, scalar1=-16.0, scalar2=0.0, op0=ALU.mult, op1=ALU.add),
             reads=[("c1", l)], writes=[("c2", l)])
        S.op("dve", lambda e, c1s=c1s: e.tensor_scalar(out=c1s, in0=c1s, scalar1=-8.0, scalar2=0.0, op0=ALU.mult, op1=ALU.add),
             reads=[("c2", l)], writes=[("c1", l)])

    total_pieces = NT * L * NPIECE
    wstate = {"emitted": 0, "conv": 0}

    def ensure_conv(upto):
        upto = min(upto, L * NPIECE - 1)
        while wstate["conv"] <= upto:
            gi = wstate["conv"]
            l, p = divmod(gi, NPIECE)
            _, _, off, n = PIECES[p]
            S.dma("pool", f"cv{gi % 4}",
                  lambda e, l=l, off=off, n=n: e.dma_start(out=wbf[l, :, off:off + n], in_=wts[l, :, off:off + n]),
                  writes=[("wbf", l, p)])
            wstate["conv"] += 1

    def prefetch(upto):
        upto = min(upto, total_pieces - 1)
        while wstate["emitted"] <= upto:
            gi = wstate["emitted"]
            tl, p = divmod(gi, NPIECE)
            l = tl % L
            _, _, off, n = PIECES[p]
            if gi < L * NPIECE:
                ensure_conv(gi + 4)
            s = gi % NS
            if PIECES[p][0] == "g":
                S.dma("sp", "wg",
                      lambda e, l=l, off=off, n=n: e.dma_start(out=wgbuf[:, 0:n], in_=wbf[l, :, off:off + n]),
                      reads=[("wbf", l, p)], writes=["wgbuf"])
            else:
                S.dma("sp", f"w{s}",
                      lambda e, l=l, off=off, n=n, s=s: e.dma_start(out=ring[s][:, 0:n], in_=wbf[l, :, off:off + n]),
                      reads=[("wbf", l, p)], writes=[("ring", s)])
            wstate["emitted"] += 1

    def wpiece(ti, l, p):
        gi = (ti * L + l) * NPIECE + p
        prefetch(gi + NS - 1)
        s = gi % NS
        if PIECES[p][0] == "g":
            return wgbuf, "wgbuf"
        return ring[s], ("ring", s)

    def mm_group(bank, items, reads, n=T):
        def fn(e, items=items, bank=bank, n=n):
            ins = None
            last = len(items) - 1
            for i, (lt, rh) in enumerate(items):
                ins = e.matmul(ps[bank][:, 0:n], lt, rh, start=(i == 0), stop=(i == last))
            return ins
        return S.op("pe", fn, reads=reads, writes=[("ps", bank)])

    def rms_stats(src_key_list, nchunks, ones_t, ones_key, rbuf, rkey):
        b = next_bank()
        mm_group(b, [(ones_t[:], sq[:, c * T:(c + 1) * T]) for c in range(nchunks)],
                 reads=[ones_key] + [("sq", c) for c in range(nchunks)])
        S.op("act", lambda e, b=b: e.activation(out=rbuf[:], in_=ps[b][:, 0:T], func=AF.Ln, bias=eps_c, scale=1.0),
             reads=["cst"], writes=[("ps", b), rkey])
        S.op("act", lambda e: e.activation(out=rbuf[:], in_=rbuf[:], func=AF.Exp, scale=-0.5), writes=[rkey])

    def pre_norm(l, goff, rb, hc, hk):
        rbuf, rkey = rstd[rb], ("rstd", rb)
        S.op("act", lambda e: e.activation(out=sq[:], in_=hc[:], func=AF.Square),
             reads=[hk(c) for c in range(NCH)], writes=[("sq", c) for c in range(NCH)])
        rms_stats(None, NCH, ones_bf, "ones_bf", rbuf, rkey)
        for c in range(NCH):
            S.op("dve", lambda e, c=c: e.scalar_tensor_tensor(
                out=z[:, c * T:(c + 1) * T], in0=hc[:, c * T:(c + 1) * T], scalar=col(l, goff, c), in1=rbuf[:],
                op0=ALU.mult, op1=ALU.mult),
                reads=[hk(c), rkey, "prm"], writes=[("z", c)])

    XL_KEYS = [("xl", m) for m in range(NCH)] + ["xlh"]
    GG_KEYS = [("gg", m) for m in range(NCH)]
    CB_KEYS = [("cb", j) for j in range(4)] + ["cbh"]
    AF_KEYS = [("affn", j) for j in range(24)]
    out_toks = []

    def layer_body(ti, l):
        t0 = ti * T
        last_layer = (l == L - 1)
        hcur, hoth = ti % 2, (ti + 1) % 2
        hc = XY[hcur]
        Rb = XY[hoth][:, 0:4 * T]
        IGb = XY[hoth][:, 4 * T:8 * T]

        def hk(c):
            return ("xy", hcur, c)

        def rk(i):
            return ("xy", hoth, i)

        def igk(i):
            return ("xy", hoth, 4 + i)

        S.alias(XL_KEYS + GG_KEYS + CB_KEYS, AF_KEYS)
        pre_norm(l, P_GPRE, 0, hc, hk)
        S.op("pool", lambda e, l=l: e.tensor_copy(out=v3(xl, NCH)[:, :, 0:3],
                                                 in_=v3(tailx[:, l * 24:(l + 1) * 24], NCH)),
             reads=[("tailx", l)], writes=["xlh"])
        S.op("pool", lambda e, l=l: e.tensor_copy(out=v3(cbuf, 4)[:, :, 0:30],
                                                 in_=v3(tailc[:, l * 120:(l + 1) * 120], 4)),
             reads=[("tailc", l)], writes=["cbh"])
        sqf = sq[:].bitcast(F32)
        LBUF = [
            dict(R=Rb, Rk=lambda i: [rk(i)], IG=IGb, IGk=lambda i: [igk(i)], M=Mb, Mk=lambda i: [("M", i)]),
            dict(R=o[:, 0:4 * T], Rk=lambda i: [("o", i)], IG=sqf, IGk=lambda i: [("sq", 2 * i), ("sq", 2 * i + 1)],
                 M=o[:, 4 * T:8 * T], Mk=lambda i: [("o", 4 + i)]),
        ]

        def in_chunk(wt, wkey, i, oc):
            b = next_bank()
            mm_group(b, [(wt[:, i * 1024 + kc * 128: i * 1024 + (kc + 1) * 128], z[:, kc * T:(kc + 1) * T])
                         for kc in range(NCH)],
                     reads=[wkey] + [("z", kc) for kc in range(NCH)])
            if oc >= 20:
                j = oc - 20
                S.op("act", lambda e, b=b, j=j: e.activation(out=sig[j % 2][:], in_=ps[b][:, 0:T], func=AF.Sigmoid),
                     writes=[("ps", b), ("sig", j % 2)])
            elif oc >= 16:
                j = oc - 16
                S.op("dve", lambda e, b=b, j=j: e.tensor_tensor(
                    out=cbuf[:, j * CBW + 30:(j + 1) * CBW], in0=ps[b][:, 0:T], in1=sig[j % 2][:], op=ALU.mult),
                    reads=[("sig", j % 2)], writes=[("ps", b), ("cb", j)])
            elif oc < 8:
                m = oc
                S.op("act", lambda e, b=b, m=m: e.activation(out=xl[:, m * XLW + 3:(m + 1) * XLW], in_=ps[b][:, 0:T], func=AF.Copy),
                     writes=[("ps", b), ("xl", m)])
            else:
                m = oc - 8
                S.op("act", lambda e, b=b, m=m: e.activation(out=gg[:, m * T:(m + 1) * T], in_=ps[b][:, 0:T], func=AF.Gelu_apprx_tanh),
                     writes=[("ps", b), ("gg", m)])

        def build_dg4(bt):
            wc4 = prm[:, l * NP + P_LCW + bt * 16: l * NP + P_LCW + bt * 16 + 16]
            S.op("dve", lambda e, wc4=wc4: e.tensor_tensor(
                out=dg4[:].rearrange("p (k c) -> p k c", k=16),
                in0=ident_bf[:].unsqueeze(1).broadcast_to([128, 16, 128]),
                in1=wc4.unsqueeze(2).broadcast_to([128, 16, 128]), op=ALU.mult),
                reads=["ident", "prm"], writes=["dg4"])

        dgslot = {}
        dgc = [0]

        def build_dg31(j, half):
            g0 = half * 16
            nk = min(16, 31 - g0)
            slot = dgc[0] % 3
            dgc[0] += 1
            dgslot[(j, half)] = slot
            wc = prm[:, l * NP + P_CW + j * 31 + g0: l * NP + P_CW + j * 31 + g0 + nk]
            S.op("dve", lambda e, slot=slot, nk=nk, wc=wc: e.tensor_tensor(
                out=dg31[slot][:, 0:nk * 128].rearrange("p (k c) -> p k c", k=nk),
                in0=ident_bf[:].unsqueeze(1).broadcast_to([128, nk, 128]),
                in1=wc.unsqueeze(2).broadcast_to([128, nk, 128]), op=ALU.mult),
                reads=["ident", "prm"], writes=[("dg31", slot)])

        cbank = {}

        def conv31_mm(j, half):
            if half == 0:
                cbank[j] = next_bank()
            b = cbank[j]
            g0 = half * 16
            ks = list(range(g0, min(g0 + 16, 31)))
            slot = dgslot[(j, half)]

            def fn(e, ks=ks, slot=slot, b=b, j=j):
                ins = None
                for kk, k in enumerate(ks):
                    ins = e.matmul(ps[b][:, 0:T], dg31[slot][:, kk * 128:(kk + 1) * 128],
                                   cbuf[:, j * CBW + k: j * CBW + k + T], start=(k == 0), stop=(k == 30))
                return ins
            S.op("pe", fn, reads=[("dg31", slot), ("cb", j), "cbh"], writes=[("ps", b)])
            if half == 1:
                S.op("act", lambda e, b=b, j=j: e.activation(out=cc[:, j * T:(j + 1) * T], in_=ps[b][:, 0:T], func=AF.Identity,
                                                             bias=col(l, P_CB, j), scale=1.0),
                     reads=["prm"], writes=[("ps", b), ("cc", j)])

        def lru_front(bt, wg, wgkey):
            LB = LBUF[bt]
            for i in range(4):
                m = bt * 4 + i
                b = next_bank()
                mm_group(b, [(dg4[:, (i * 4 + k) * 128:(i * 4 + k + 1) * 128], xl[:, m * XLW + k: m * XLW + k + T]) for k in range(4)],
                         reads=["dg4", ("xl", m), "xlh"])
                S.op("act", lambda e, b=b, i=i, m=m: e.activation(out=xcb[:, i * T:(i + 1) * T], in_=ps[b][:, 0:T], func=AF.Identity,
                                                                 bias=col(l, P_LCB, m), scale=1.0),
                     reads=["prm"], writes=[("ps", b), ("xcb", i)])
                S.op("act", lambda e, b=b, i=i, m=m: e.activation(out=xcf[:, i * T:(i + 1) * T], in_=ps[b][:, 0:T], func=AF.Identity,
                                                                 bias=col(l, P_LCB, m), scale=1.0),
                     reads=["prm"], writes=[("ps", b), ("xcf", i)])
            for i in range(4):
                m = bt * 4 + i
                br = next_bank()
                mm_group(br, [(wg[:, m * 128:(m + 1) * 128], xcb[:, i * T:(i + 1) * T])], reads=[wgkey, ("xcb", i)])
                S.op("act", lambda e, br=br, i=i, m=m, LB=LB: e.activation(out=LB["R"][:, i * T:(i + 1) * T], in_=ps[br][:, 0:T],
                                                                         func=AF.Sigmoid, bias=col(l, P_BA, m), scale=1.0),
                     reads=["prm"], writes=[("ps", br)] + LB["Rk"](i))
                bi = next_bank()
                mm_group(bi, [(wg[:, 1024 + m * 128:1024 + (m + 1) * 128], xcb[:, i * T:(i + 1) * T])], reads=[wgkey, ("xcb", i)])
                S.op("act", lambda e, bi=bi, i=i, m=m, LB=LB: e.activation(out=LB["IG"][:, i * T:(i + 1) * T], in_=ps[bi][:, 0:T],
                                                                         func=AF.Sigmoid, bias=col(l, P_BX, m), scale=1.0),
                     reads=["prm"], writes=[("ps", bi)] + LB["IGk"](i))

        def lru_act_tail(bt):
            LB = LBUF[bt]
            for i in range(4):
                m = bt * 4 + i
                S.op("act", lambda e, i=i, m=m, LB=LB: e.activation(out=LB["M"][:, i * T:(i + 1) * T], in_=LB["R"][:, i * T:(i + 1) * T],
                                                                   func=AF.Exp, scale=cst[:, C2 + l * NCH + m:C2 + l * NCH + m + 1]),
                     reads=LB["Rk"](i) + [("c2", l)], writes=LB["Mk"](i))
            for i in range(4):
                m = bt * 4 + i
                S.op("act", lambda e, i=i, m=m, LB=LB: e.activation(out=LB["R"][:, i * T:(i + 1) * T], in_=LB["R"][:, i * T:(i + 1) * T],
                                                                   func=AF.Exp, scale=cst[:, C1 + l * NCH + m:C1 + l * NCH + m + 1]),
                     reads=[("c1", l)], writes=LB["Rk"](i))
            for i in range(4):
                S.op("act", lambda e, i=i, LB=LB: e.activation(out=LB["M"][:, i * T:(i + 1) * T], in_=LB["M"][:, i * T:(i + 1) * T],
                                                              func=AF.Sqrt, bias=one_c, scale=-1.0),
                     reads=["cst1"], writes=LB["Mk"](i))

        def lru_dve_a(bt):
            LB = LBUF[bt]
            for i in range(4):
                S.op("dve", lambda e, i=i, LB=LB: e.tensor_tensor(out=LB["IG"][:, i * T:(i + 1) * T], in0=LB["IG"][:, i * T:(i + 1) * T],
                                                                 in1=xcf[:, i * T:(i + 1) * T], op=ALU.mult),
                     reads=[("xcf", i)], writes=LB["IGk"](i))
            for i in range(4):
                S.op("dve", lambda e, i=i, LB=LB: e.tensor_tensor(out=LB["M"][:, i * T:(i + 1) * T], in0=LB["M"][:, i * T:(i + 1) * T],
                                                                 in1=LB["IG"][:, i * T:(i + 1) * T], op=ALU.mult),
                     reads=LB["IGk"](i), writes=LB["Mk"](i))

        def lru_dve_b(bt):
            LB = LBUF[bt]
            for i in range(4):
                m = bt * 4 + i
                hb = i % 2
                S.op("dve", lambda e, i=i, m=m, hb=hb, LB=LB: e.tensor_tensor_scan(
                    out=hl[hb][:], data0=LB["R"][:, i * T:(i + 1) * T], data1=LB["M"][:, i * T:(i + 1) * T],
                    initial=state[:, l * NCH + m:l * NCH + m + 1], op0=ALU.mult, op1=ALU.add),
                    reads=LB["Rk"](i) + LB["Mk"](i) + [("st", l, m)], writes=[("hl", hb)])
                S.op("pool", lambda e, m=m, hb=hb: e.tensor_copy(out=state[:, l * NCH + m:l * NCH + m + 1], in_=hl[hb][:, T - 1:T]),
                     reads=[("hl", hb)], writes=[("st", l, m)])
                S.op("dve", lambda e, m=m, hb=hb: e.tensor_tensor(out=y[:, m * T:(m + 1) * T], in0=hl[hb][:],
                                                                 in1=gg[:, m * T:(m + 1) * T], op=ALU.mult),
                     reads=[("hl", hb), ("gg", m)], writes=[("y", m)])

        for q in range(2):
            wt, wkey = wpiece(ti, l, q)
            for i in range(4):
                in_chunk(wt, wkey, i, IN_ORDER[q * 4 + i])
        S.op("pool", lambda e, l=l: e.tensor_copy(out=v3(tailx[:, l * 24:(l + 1) * 24], NCH),
                                                 in_=v3(xl, NCH)[:, :, T:T + 3]),
             reads=[("xl", m) for m in range(NCH)], writes=[("tailx", l)])
        wg, wgkey = wpiece(ti, l, 2)
        build_dg4(0)
        build_dg31(0, 0)
        build_dg31(0, 1)
        build_dg31(1, 0)
        lru_front(0, wg, wgkey)
        lru_act_tail(0)
        build_dg4(1)
        for q in range(2):
            wt, wkey = wpiece(ti, l, 3 + q)
            for i in range(4):
                in_chunk(wt, wkey, i, IN_ORDER[16 + q * 4 + i])
        lru_dve_a(0)
        lru_front(1, wg, wgkey)
        lru_dve_b(0)
        lru_act_tail(1)
        for q in range(2):
            wt, wkey = wpiece(ti, l, 5 + q)
            for i in range(4):
                in_chunk(wt, wkey, i, IN_ORDER[8 + q * 4 + i])
        S.op("pool", lambda e, l=l: e.tensor_copy(out=v3(tailc[:, l * 120:(l + 1) * 120], 4),
                                                 in_=v3(cbuf, 4)[:, :, T:T + 30]),
             reads=[("cb", j) for j in range(4)], writes=[("tailc", l)])
        conv31_mm(0, 0)
        build_dg31(1, 1)
        conv31_mm(0, 1)
        conv31_mm(1, 0)
        build_dg31(2, 0)
        lru_dve_a(1)
        conv31_mm(1, 1)
        build_dg31(2, 1)
        conv31_mm(2, 0)
        build_dg31(3, 0)
        lru_dve_b(1)
        conv31_mm(2, 1)
        build_dg31(3, 1)
        conv31_mm(3, 0)
        conv31_mm(3, 1)
        S.op("act", lambda e: e.activation(out=sq[:, 0:8 * T], in_=y[:, 0:8 * T], func=AF.Square),
             reads=[("y", m) for m in range(8)], writes=[("sq", c) for c in range(8)])
        rms_stats(None, 8, ones_bf, "ones_bf", rstd[0], ("rstd", 0))
        for m in range(8):
            S.op("dve", lambda e, m=m: e.scalar_tensor_tensor(
                out=y[:, m * T:(m + 1) * T], in0=y[:, m * T:(m + 1) * T], scalar=col(l, P_GOL, m), in1=rstd[0][:],
                op0=ALU.mult, op1=ALU.mult), reads=[("rstd", 0), "prm"], writes=[("y", m)])
        bd = []
        for j in range(4):
            bj = next_bank()
            bd.append(bj)
            mm_group(bj, [(cen_f[:], cc[:, j * T:(j + 1) * T])], reads=["cen_f", ("cc", j)])
        for j in range(4):
            S.op("act", lambda e, j=j, bj=bd[j]: e.activation(out=IGb[:, j * T:(j + 1) * T], in_=ps[bj][:, 0:T], func=AF.Square),
                 writes=[("ps", bd[j]), igk(j)])
        bv = []
        for j in range(4):
            bj = next_bank()
            bv.append(bj)
            mm_group(bj, [(ones_f[:], IGb[:, j * T:(j + 1) * T])], reads=["ones_f", igk(j)])
        for j in range(4):
            S.op("act", lambda e, j=j, bj=bv[j]: e.activation(out=Rb[:, j * T:(j + 1) * T], in_=ps[bj][:, 0:T], func=AF.Ln,
                                                            bias=eps_c, scale=1.0),
                 reads=["cst"], writes=[("ps", bv[j]), rk(j)])
        for j in range(4):
            S.op("act", lambda e, j=j: e.activation(out=Rb[:, j * T:(j + 1) * T], in_=Rb[:, j * T:(j + 1) * T], func=AF.Exp, scale=-0.5),
                 writes=[rk(j)])
        for j in range(4):
            S.op("dve", lambda e, j=j, bj=bd[j]: e.tensor_tensor(out=IGb[:, j * T:(j + 1) * T], in0=ps[bj][:, 0:T],
                                                               in1=Rb[:, j * T:(j + 1) * T], op=ALU.mult),
                 reads=[rk(j)], writes=[("ps", bd[j]), igk(j)])
        for j in range(4):
            S.op("act", lambda e, j=j: e.activation(out=y[:, (8 + j) * T:(9 + j) * T], in_=IGb[:, j * T:(j + 1) * T], func=AF.Silu,
                                                   scale=col(l, P_LNG, j), bias=col(l, P_LNB, j)),
                 reads=[igk(j), "prm"], writes=[("y", 8 + j)])
        S.op("act", lambda e: e.activation(out=sq[:, 0:4 * T], in_=y[:, 8 * T:12 * T], func=AF.Square),
             reads=[("y", 8 + j) for j in range(4)], writes=[("sq", c) for c in range(4)])
        rms_stats(None, 4, ones5_bf, "ones5_bf", rstd[1], ("rstd", 1))
        for j in range(4):
            S.op("dve", lambda e, j=j: e.scalar_tensor_tensor(
                out=y[:, (8 + j) * T:(9 + j) * T], in0=y[:, (8 + j) * T:(9 + j) * T], scalar=col(l, P_GOC, j), in1=rstd[1][:],
                op0=ALU.mult, op1=ALU.mult), reads=[("rstd", 1), "prm"], writes=[("y", 8 + j)])
        for q in range(4):
            wt, wkey = wpiece(ti, l, 7 + q)
            for i in range(2):
                m = q * 2 + i
                b = next_bank()
                mm_group(b, [(wt[:, i * 1536 + kc * 128: i * 1536 + (kc + 1) * 128], y[:, kc * T:(kc + 1) * T])
                             for kc in range(12)],
                         reads=[wkey] + [("y", kc) for kc in range(12)])
                S.op("act", lambda e, b=b, m=m: e.activation(out=o[:, m * T:(m + 1) * T], in_=ps[b][:, 0:T], func=AF.Copy),
                     writes=[("ps", b), ("o", m)])
                S.op("act", lambda e, b=b, m=m: e.activation(out=sq[:, m * T:(m + 1) * T], in_=ps[b][:, 0:T], func=AF.Square),
                     writes=[("ps", b), ("sq", m)])
        rms_stats(None, NCH, ones_bf, "ones_bf", rstd[0], ("rstd", 0))
        for m in range(NCH):
            S.op("dve", lambda e, m=m: e.scalar_tensor_tensor(
                out=o[:, m * T:(m + 1) * T], in0=o[:, m * T:(m + 1) * T], scalar=col(l, P_GPM, m), in1=rstd[0][:],
                op0=ALU.mult, op1=ALU.mult), reads=[("rstd", 0), "prm"], writes=[("o", m)])
        for m in range(NCH):
            S.op("dve", lambda e, m=m: e.tensor_tensor(out=hc[:, m * T:(m + 1) * T], in0=hc[:, m * T:(m + 1) * T],
                                                      in1=o[:, m * T:(m + 1) * T], op=ALU.add),
                 reads=[("o", m)], writes=[hk(m)])
        S.alias(AF_KEYS, XL_KEYS + GG_KEYS + CB_KEYS)
        if last_layer and ti + 1 < NT:
            t1 = (ti + 1) * T
            S.dma("act", "hld",
                  lambda e, t1=t1, hoth=hoth: e.dma_start(out=v3(XY[hoth][:], NCH), in_=hT[:, :, t1:t1 + T].rearrange("c p t -> p c t")),
                  writes=[("xy", hoth, c) for c in range(NCH)])
        pre_norm(l, P_GPF, 1, hc, hk)
        def ffn_a(j):
            q, i = divmod(j, 2)
            wt, wkey = wpiece(ti, l, 11 + q)
            s = j % 2
            yb = ybuf[s]
            S.op("pool", lambda e, yb=yb, j=j: e.tensor_copy(
                out=v3(yb[:], 2)[:, :, 0:2], in_=v3(taily[:, (l * 24 + j) * 4:(l * 24 + j + 1) * 4], 2)),
                reads=[("taily", l, j)], writes=[("ybh", s)])
            tb = []
            for gu in range(2):
                b = next_bank()
                base = i * 2048 + gu * 1024
                mm_group(b, [(wt[:, base + kc * 128: base + (kc + 1) * 128], z[:, kc * T:(kc + 1) * T])
                             for kc in range(NCH)],
                         reads=[wkey] + [("z", kc) for kc in range(NCH)])
                idx = j * 2 + gu
                tt = tbuf[s * 2 + gu]
                tk = ("tb", s * 2 + gu)
                S.op("act", lambda e, b=b, yb=yb, gu=gu: e.activation(out=yb[:, gu * YBW + 2:(gu + 1) * YBW], in_=ps[b][:, 0:T], func=AF.Copy),
                     writes=[("ps", b), ("yb", s, gu)])
                S.op("act", lambda e, b=b, tt=tt, idx=idx: e.activation(out=tt[:], in_=ps[b][:, 0:T], func=AF.Identity,
                                                                        scale=col(l, P_FCW, 2 * 48 + idx), bias=col(l, P_FCB, idx)),
                     reads=["prm"], writes=[("ps", b), tk])
                tb.append((tt, tk, idx))
            S.op("pool", lambda e, yb=yb, j=j: e.tensor_copy(
                out=v3(taily[:, (l * 24 + j) * 4:(l * 24 + j + 1) * 4], 2), in_=v3(yb[:], 2)[:, :, T:T + 2]),
                reads=[("yb", s, 0), ("yb", s, 1)], writes=[("taily", l, j)])
            for k in (1, 0):
                for gu in range(2):
                    tt, tk, idx = tb[gu]
                    S.op("dve", lambda e, yb=yb, gu=gu, tt=tt, idx=idx, k=k: e.scalar_tensor_tensor(
                        out=tt[:], in0=yb[:, gu * YBW + k: gu * YBW + k + T], scalar=col(l, P_FCW, k * 48 + idx), in1=tt[:],
                        op0=ALU.mult, op1=ALU.add),
                        reads=[("yb", s, gu), ("ybh", s), "prm"], writes=[tk])
            return tb

        def ffn_b(j, tb):
            s = j % 2
            S.op("act", lambda e, s=s, tt=tb[0][0]: e.activation(out=gl[s][:], in_=tt[:], func=AF.Gelu_apprx_tanh),
                 reads=[tb[0][1]], writes=[("gl", s)])
            S.op("dve", lambda e, s=s, j=j, tt=tb[1][0]: e.tensor_tensor(out=affn[:, j * T:(j + 1) * T], in0=gl[s][:], in1=tt[:], op=ALU.mult),
                 reads=[("gl", s), tb[1][1]], writes=[("affn", j)])

        if FFN_PIPE:
            prev = ffn_a(0)
            for j in range(1, 24):
                cur = ffn_a(j)
                ffn_b(j - 1, prev)
                prev = cur
            ffn_b(23, prev)
        else:
            for j in range(24):
                ffn_b(j, ffn_a(j))
        for m in range(NCH):
            wt, wkey = wpiece(ti, l, 23 + m)
            b = next_bank()
            mm_group(b, [(wt[:, kc * 128:(kc + 1) * 128], affn[:, kc * T:(kc + 1) * T]) for kc in range(24)],
                     reads=[wkey] + AF_KEYS)
            S.op("act", lambda e, b=b, m=m: e.activation(out=o[:, m * T:(m + 1) * T], in_=ps[b][:, 0:T], func=AF.Copy),
                 writes=[("ps", b), ("o", m)])
            S.op("act", lambda e, b=b, m=m: e.activation(out=sq[:, m * T:(m + 1) * T], in_=ps[b][:, 0:T], func=AF.Square),
                 writes=[("ps", b), ("sq", m)])
        rms_stats(None, NCH, ones_bf, "ones_bf", rstd[0], ("rstd", 0))
        for m in range(NCH):
            S.op("dve", lambda e, m=m: e.scalar_tensor_tensor(
                out=o[:, m * T:(m + 1) * T], in0=o[:, m * T:(m + 1) * T], scalar=col(l, P_GPO, m), in1=rstd[0][:],
                op0=ALU.mult, op1=ALU.mult), reads=[("rstd", 0), "prm"], writes=[("o", m)])
        for m in range(NCH):
            if last_layer:
                S.op("dve", lambda e, m=m: e.tensor_tensor(out=o[:, m * T:(m + 1) * T], in0=o[:, m * T:(m + 1) * T],
                                                          in1=hc[:, m * T:(m + 1) * T], op=ALU.add),
                     reads=[hk(m)], writes=[("o", m)])
            else:
                S.op("dve", lambda e, m=m: e.tensor_tensor(out=hc[:, m * T:(m + 1) * T], in0=hc[:, m * T:(m + 1) * T],
                                                          in1=o[:, m * T:(m + 1) * T], op=ALU.add),
                     reads=[("o", m)], writes=[hk(m)])

    for ti in range(NT):
        t0 = ti * T
        if ti == 0:
            S.dma("act", "hld",
                  lambda e: e.dma_start(out=v3(XY[0][:], NCH), in_=hT[:, :, 0:T].rearrange("c p t -> p c t")),
                  writes=[("xy", 0, c) for c in range(NCH)])
        for l in range(L):
            layer_body(ti, l)
        a = NMETA if ti == 0 else 0
        d0 = t0 + a - NMETA
        n = T - a
        out_toks.append(S.dma(
            "act", "st",
            lambda e, a=a, d0=d0, n=n: e.dma_start(out=oT[:, :, d0:d0 + n].rearrange("c p t -> p c t"),
                                                   in_=v3(o[:], NCH)[:, :, a:T]),
            reads=[("o", m) for m in range(NCH)]))
    S.final_wait("act", out_toks)
    if emit:
        S.emit()
    return nc, S


def _cols(v, nch):
    return np.ascontiguousarray(np.asarray(v, np.float32).reshape(nch, 128).T)


def pack_weights(w_in, lru_wa, lru_wx, w_out, w_up, w_down, L):
    out = np.empty((L, 128, WCOLS), np.float32)
    for l in range(L):
        a = np.asarray(w_in[l], np.float32).reshape(8, 128, 24, 128)[:, :, IN_ORDER, :]
        out[l, :, OFF_IN:OFF_G] = a.transpose(1, 2, 0, 3).reshape(128, -1)
        out[l, :, OFF_G:OFF_G + 1024] = np.asarray(lru_wa[l], np.float32).transpose(1, 0, 2).reshape(128, -1)
        out[l, :, OFF_G + 1024:OFF_OUT] = np.asarray(lru_wx[l], np.float32).transpose(1, 0, 2).reshape(128, -1)
        a = np.asarray(w_out[l], np.float32).reshape(12, 128, 8, 128)
        out[l, :, OFF_OUT:OFF_UP] = a.transpose(1, 2, 0, 3).reshape(128, -1)
        a = np.asarray(w_up[l], np.float32).reshape(8, 128, 2, 24, 128)
        out[l, :, OFF_UP:OFF_DOWN] = a.transpose(1, 3, 2, 0, 4).reshape(128, -1)
        a = np.asarray(w_down[l], np.float32).reshape(24, 128, 8, 128)
        out[l, :, OFF_DOWN:WCOLS] = a.transpose(1, 2, 0, 3).reshape(128, -1)
    return out


def pack_params(inp, L):
    prm = np.zeros((128, L * NP), np.float32)
    for l in range(L):
        b = l * NP
        prm[:, b + P_GPRE:b + P_GPRE + 8] = _cols(inp["g_pre_mix"][l], 8)
        a = np.asarray(inp["lru_conv_w"][l], np.float32).reshape(4, 8, 128)
        prm[:, b + P_LCW:b + P_LCW + 32] = a.transpose(2, 1, 0).reshape(128, 32)
        prm[:, b + P_LCB:b + P_LCB + 8] = _cols(inp["lru_conv_b"][l], 8)
        prm[:, b + P_BA:b + P_BA + 8] = _cols(inp["lru_ba"][l], 8)
        prm[:, b + P_BX:b + P_BX + 8] = _cols(inp["lru_bx"][l], 8)
        prm[:, b + P_LAM:b + P_LAM + 8] = _cols(inp["lru_lambda"][l], 8)
        a = np.asarray(inp["conv_w"][l], np.float32).reshape(31, 4, 128)
        prm[:, b + P_CW:b + P_CW + 124] = a.transpose(2, 1, 0).reshape(128, 124)
        prm[:, b + P_CB:b + P_CB + 4] = _cols(inp["conv_b"][l], 4)
        prm[:, b + P_LNG:b + P_LNG + 4] = _cols(inp["conv_ln_g"][l], 4)
        prm[:, b + P_LNB:b + P_LNB + 4] = _cols(inp["conv_ln_b"][l], 4)
        prm[:, b + P_GOL:b + P_GOL + 8] = _cols(inp["g_out_lru"][l], 8)
        prm[:, b + P_GOC:b + P_GOC + 4] = _cols(inp["g_out_conv"][l], 4)
        prm[:, b + P_GPM:b + P_GPM + 8] = _cols(inp["g_post_mix"][l], 8)
        prm[:, b + P_GPF:b + P_GPF + 8] = _cols(inp["g_pre_ffn"][l], 8)
        for k in range(3):
            a = np.asarray(inp["ffn_conv_w"][l][k], np.float32).reshape(2, 24, 128)
            prm[:, b + P_FCW + k * 48:b + P_FCW + (k + 1) * 48] = a.transpose(2, 1, 0).reshape(128, 48)
        a = np.asarray(inp["ffn_conv_b"][l], np.float32).reshape(2, 24, 128)
        prm[:, b + P_FCB:b + P_FCB + 48] = a.transpose(2, 1, 0).reshape(128, 48)
        prm[:, b + P_GPO:b + P_GPO + 8] = _cols(inp["g_post_ffn"][l], 8)
    return prm


_CACHE = {}


def run_cores(x, meta_tokens, inp, NT, L, n_cores):
    B, S_, _ = x.shape
    key = (NT, L)
    if key not in _CACHE:
        _CACHE[key] = build_nc(NT, L)[0]
    nc = _CACHE[key]
    wts = pack_weights(inp["w_in"], inp["lru_wa"], inp["lru_wx"], inp["w_out"], inp["w_up"], inp["w_down"], L)
    prm = pack_params(inp, L)
    meta = np.asarray(meta_tokens, np.float32)
    in_maps = []
    for b in range(n_cores):
        hfull = np.concatenate([meta, np.asarray(x[b], np.float32)], axis=0)
        hT = np.ascontiguousarray(hfull.T).reshape(NCH, 128, NT * T)
        in_maps.append({"hT": hT, "wts": wts, "prm": prm, "ident": np.eye(128, dtype=np.float32),
                        "cen": (np.eye(128) - 1.0 / 128.0).astype(np.float32)})
    res = run_bass_kernel_spmd(nc, in_maps, core_ids=list(range(n_cores)))
    outs = []
    for b in range(n_cores):
        oT = np.asarray(res.results[b]["oT"]).reshape(D, NT * T - NMETA)
        outs.append(np.ascontiguousarray(oT.T))
    return np.stack(outs, axis=0)


def kernel(**inputs):
    x = np.asarray(inputs["x"], np.float32)
    out = run_cores(x, inputs["meta_tokens"], inputs, NT_FULL, DEPTH, 8)
    return out.astype(np.float32)
```

```python
import numpy as np
import concourse.bass as bass
import concourse.mybir as mybir
from concourse.bass_utils import run_bass_kernel_spmd

F32 = mybir.dt.float32
BF16 = mybir.dt.bfloat16
AF = mybir.ActivationFunctionType
ALU = mybir.AluOpType

D = 1024
SEQ = 8192
NMETA = 16
STOT = SEQ + NMETA
T = 456
NT_FULL = STOT // T
DEPTH = 4
NCH = 8
EPS = 1e-6
NS = 4
FFN_PIPE = 1
NO_SELF_WAIT = 1
SLOT = 4096

OFF_IN, OFF_G, OFF_OUT, OFF_UP, OFF_DOWN = 0, 24576, 26624, 38912, 88064
WCOLS = 112640
IN_ORDER = list(range(0, 8)) + [20, 16, 21, 17, 22, 18, 23, 19] + list(range(8, 16))
PIECES = ([("in", q, OFF_IN + q * 4096, 4096) for q in range(2)]
          + [("g", 0, OFF_G, 2048)]
          + [("in", q, OFF_IN + q * 4096, 4096) for q in (4, 5, 2, 3)]
          + [("out", q, OFF_OUT + q * 3072, 3072) for q in range(4)]
          + [("up", q, OFF_UP + q * 4096, 4096) for q in range(12)]
          + [("down", q, OFF_DOWN + q * 3072, 3072) for q in range(8)])
NPIECE = len(PIECES)

P_GPRE, P_LCW, P_LCB, P_BA, P_BX, P_LAM, P_CW, P_CB, P_LNG, P_LNB = 0, 8, 40, 48, 56, 64, 72, 196, 200, 204
P_GOL, P_GOC, P_GPM, P_GPF, P_FCW, P_FCB, P_GPO = 208, 216, 220, 228, 236, 380, 428
NP = 436

ENGS = ("pe", "act", "dve", "pool", "sp")


class Sched:
    def __init__(self, nc, needed=None):
        self.nc = nc
        self.needed = needed
        self.targets = {e: set() for e in ENGS}
        self.streams = {e: [] for e in ENGS}
        self.count = {e: 0 for e in ENGS}
        self.waited = {e: {} for e in ENGS}
        self.res = {}
        self.dma_count = {}
        self.sem_names = set(ENGS)
        self.dummy = {}
        self.nwaits = 0
        self.ndummy = 0
        self.nops = 0

    def _deps(self, reads, writes):
        deps = {}
        for k in reads:
            r = self.res.get(k)
            if r is not None and r[0] is not None:
                s, v = r[0]
                if deps.get(s, -1) < v:
                    deps[s] = v
        for k in writes:
            r = self.res.get(k)
            if r is not None:
                if r[0] is not None:
                    s, v = r[0]
                    if deps.get(s, -1) < v:
                        deps[s] = v
                for s, v in r[1].items():
                    if deps.get(s, -1) < v:
                        deps[s] = v
        return deps

    def _commit(self, reads, writes, tok):
        s, v = tok
        for k in reads:
            r = self.res.get(k)
            if r is None:
                r = self.res[k] = [None, {}]
            if r[1].get(s, -1) < v:
                r[1][s] = v
        for k in writes:
            self.res[k] = [tok, {}]

    def _mkwaits(self, eng, deps):
        w = []
        wd = self.waited[eng]
        for s, v in deps.items():
            if wd.get(s, -1) >= v:
                continue
            wd[s] = v
            w.append((s, v))
            if s in self.targets:
                self.targets[s].add(v)
        self.nwaits += len(w)
        return w

    def op(self, eng, fn, reads=(), writes=()):
        deps = self._deps(reads, writes)
        if eng == "pe":
            deps.pop("pe", None)
        st = self.streams[eng]
        if (deps.get(eng, -1) == self.count[eng] and self.waited[eng].get(eng, -1) < self.count[eng]
                and eng in self.dummy and st and st[-1][2] == (eng, 1)):
            st.append(([], self.dummy[eng], None, 0))
            self.ndummy += 1
        if NO_SELF_WAIT and eng in ("act", "dve"):
            deps.pop(eng, None)
        waits = self._mkwaits(eng, deps)
        self.count[eng] += 1
        tok = (eng, self.count[eng])
        st.append((waits, fn, (eng, 1), self.count[eng]))
        self._commit(reads, writes, tok)
        self.nops += 1
        return tok

    def dma(self, eng, chan, fn, reads=(), writes=()):
        writes = list(writes) + [("chan", chan)]
        deps = self._deps(reads, writes)
        waits = self._mkwaits(eng, deps)
        n = self.dma_count.get(chan, 0) + 1
        self.dma_count[chan] = n
        self.sem_names.add(chan)
        tok = (chan, 16 * n)
        self.streams[eng].append((waits, fn, (chan, 16), 0))
        self._commit(reads, writes, tok)
        return tok

    def alias(self, new_keys, old_keys):
        merged = {}
        for k in old_keys:
            r = self.res.get(k)
            if r is None:
                continue
            if r[0] is not None:
                s, v = r[0]
                if merged.get(s, -1) < v:
                    merged[s] = v
            for s, v in r[1].items():
                if merged.get(s, -1) < v:
                    merged[s] = v
        for k in new_keys:
            self.res[k] = [None, dict(merged)]

    def final_wait(self, eng, toks):
        deps = {}
        for s, v in toks:
            if deps.get(s, -1) < v:
                deps[s] = v
        self.streams[eng].append((self._mkwaits(eng, deps), None, None, 0))

    def emit(self):
        nc = self.nc
        sems = {n: nc.alloc_semaphore("s_" + n) for n in sorted(self.sem_names)}
        streams = self.streams

        needed = self.needed
        rank = {}
        if needed is not None:
            for e in ENGS:
                rank[e] = {v: i + 1 for i, v in enumerate(sorted(needed[e]))}

        def run(engobj, name):
            for waits, fn, inc, idx in streams[name]:
                for s, v in waits:
                    if needed is not None and s in rank:
                        v = rank[s][v]
                    engobj.wait_ge(sems[s], v)
                if fn is not None:
                    ins = fn(engobj)
                    if inc is not None:
                        if needed is not None and inc[0] in rank and idx not in rank[inc[0]]:
                            continue
                        ins.then_inc(sems[inc[0]], inc[1])

        with nc.Block() as block:
            @block.tensor
            def _(e):
                run(e, "pe")

            @block.scalar
            def _(e):
                run(e, "act")

            @block.vector
            def _(e):
                run(e, "dve")

            @block.gpsimd
            def _(e):
                run(e, "pool")

            @block.sync
            def _(e):
                run(e, "sp")


def build_nc(NT=NT_FULL, L=DEPTH, dbg=None):
    _, S1 = _build(NT, L, None, emit=False)
    return _build(NT, L, S1.targets, emit=True)


def _build(NT, L, needed, emit):
    nc = bass.Bass("TRN2", target_bir_lowering=False)
    ntok = NT * T
    hT = nc.dram_tensor("hT", [NCH, 128, ntok], F32, kind="ExternalInput").ap()
    wts = nc.dram_tensor("wts", [L, 128, WCOLS], F32, kind="ExternalInput").ap()
    prm_d = nc.dram_tensor("prm", [128, L * NP], F32, kind="ExternalInput").ap()
    ident_d = nc.dram_tensor("ident", [128, 128], F32, kind="ExternalInput").ap()
    cen_d = nc.dram_tensor("cen", [128, 128], F32, kind="ExternalInput").ap()
    oT = nc.dram_tensor("oT", [NCH, 128, ntok - NMETA], F32, kind="ExternalOutput").ap()
    wbf = nc.dram_tensor("wbf", [L, 128, WCOLS], BF16, kind="Internal").ap()

    S = Sched(nc, needed)
    A = nc.alloc_sbuf_tensor

    prm = A("prm_sb", [128, L * NP], F32)
    cst = A("cst", [128, 8 + 2 * L * NCH], F32)
    ones_bf = A("ones_bf", [128, 128], BF16)
    ones5_bf = A("ones5_bf", [128, 128], BF16)
    ones_f = A("ones_f", [128, 128], F32)
    XY = [A("hx", [128, NCH * T], F32), A("hy", [128, NCH * T], F32)]
    z = A("z", [128, NCH * T], BF16)
    sq = A("sq", [128, NCH * T], BF16)
    rstd = [A(f"rstd{i}", [128, T], F32) for i in range(2)]
    o = A("o", [128, NCH * T], F32)
    XLW = 3 + T
    CBW = 30 + T
    ar = A("arena", [128, 5604], F32)
    xl = ar[:, 0:1836].bitcast(BF16)
    gg = ar[:, 1836:3660].bitcast(BF16)
    cbuf = ar[:, 3660:4632].bitcast(BF16)
    affn = ar[:, 0:5472].bitcast(BF16)
    cc = A("cc", [128, 4 * T], F32)
    Mb = A("Mb", [128, 4 * T], F32)
    xcf = A("xcf", [128, 4 * T], F32)
    xcb = A("xcb", [128, 4 * T], BF16)
    hl = [A(f"hl{i}", [128, T], F32) for i in range(2)]
    y = A("y", [128, 12 * T], BF16)
    sig = [A(f"sig{i}", [128, T], F32) for i in range(2)]
    cen_f = A("cen_f", [128, 128], F32)
    ident_bf = A("ident_bf", [128, 128], BF16)
    dg31 = [A(f"dg31_{i}", [128, 2048], BF16) for i in range(3)]
    dg4 = A("dg4", [128, 2048], BF16)
    YBW = 2 + T
    ybuf = [A(f"ybuf{i}", [128, 2 * YBW], BF16) for i in range(2)]
    tbuf = [A(f"tbuf{i}", [128, T], F32) for i in range(4)]
    gl = [A(f"gl{i}", [128, T], BF16) for i in range(2)]
    tailx = A("tailx", [128, L * NCH * 3], BF16)
    tailc = A("tailc", [128, L * 4 * 30], BF16)
    taily = A("taily", [128, L * 48 * 2], BF16)
    state = A("state", [128, L * NCH], F32)
    ring = [A(f"ring{i}", [128, SLOT], BF16) for i in range(NS)]
    wgbuf = A("wgbuf", [128, 2048], BF16)
    ps = [nc.alloc_psum_tensor(f"ps{i}", [128, 512], F32) for i in range(8)]

    S.dummy["act"] = lambda e: e.activation(out=cst[:, 2:3], in_=cst[:, 3:4], func=AF.Copy)
    S.dummy["dve"] = lambda e: e.tensor_copy(out=cst[:, 4:5], in_=cst[:, 5:6])
    S.dummy["pool"] = lambda e: e.tensor_copy(out=cst[:, 6:7], in_=cst[:, 7:8])

    def v3(ap2, c):
        return ap2.rearrange("p (c t) -> p c t", c=c)

    def col(l, off, i=0):
        b = l * NP + off + i
        return prm[:, b:b + 1]

    bank_ctr = [0]

    def next_bank():
        b = bank_ctr[0] % 8
        bank_ctr[0] += 1
        return b

    S.dma("sp", "ldp", lambda e: e.dma_start(out=prm[:], in_=prm_d), writes=["prm"])
    S.dma("pool", "ldi", lambda e: e.dma_start(out=ident_bf[:], in_=ident_d), writes=["ident"])
    S.dma("sp", "ldc", lambda e: e.dma_start(out=cen_f[:], in_=cen_d), writes=["cen_f"])
    S.op("pool", lambda e: e.memset(cst[:, 0:1], EPS), writes=["cst"])
    S.op("pool", lambda e: e.memset(cst[:, 1:2], 1.0), writes=["cst1"])
    S.op("pool", lambda e: e.memset(cst[:, 2:8], 0.0), writes=["cstd"])
    S.op("pool", lambda e: e.memset(ones_bf[:], 1.0 / 1024.0), writes=["ones_bf"])
    S.op("pool", lambda e: e.memset(ones5_bf[:], 1.0 / 512.0), writes=["ones5_bf"])
    S.op("pool", lambda e: e.memset(ones_f[:], 1.0 / 128.0), writes=["ones_f"])
    S.op("pool", lambda e: e.memset(tailx[:], 0.0), writes=[("tailx", l) for l in range(L)])
    S.op("pool", lambda e: e.memset(tailc[:], 0.0), writes=[("tailc", l) for l in range(L)])
    S.op("pool", lambda e: e.memset(taily[:], 0.0), writes=[("taily", l, j) for l in range(L) for j in range(24)])
    S.op("pool", lambda e: e.memset(state[:], 0.0), writes=[("st", l, m) for l in range(L) for m in range(NCH)])
    eps_c = cst[:, 0:1]
    one_c = cst[:, 1:2]
    C1 = 8
    C2 = 8 + L * NCH
    for l in range(L):
        c1s = cst[:, C1 + l * NCH:C1 + (l + 1) * NCH]
        c2s = cst[:, C2 + l * NCH:C2 + (l + 1) * NCH]
        lam = prm[:, l * NP + P_LAM:l * NP + P_LAM + NCH]
        S.op("act", lambda e, c1s=c1s, lam=lam: e.activation(out=c1s, in_=lam, func=AF.Exp, scale=-1.0),
             reads=["prm"], writes=[("c1", l)])
        S.op("act", lambda e, c1s=c1s: e.activation(out=c1s, in_=c1s, func=AF.Ln, bias=one_c, scale=1.0),
             reads=["cst1"], writes=[("c1", l)])
        S.op("dve", lambda e, c1s=c1s, c2s=c2s: e.tensor_scalar(out=c2s, in0=c1s, scalar1=-16.0, scalar2=0.0, op0=ALU.mult, op1=ALU.add),
             reads=[("c1", l)], writes=[("c2", l)])
        S.op("dve", lambda e, c1s=c1s: e.tensor_scalar(out=c1s, in0=c1s, scalar1=-8.0, scalar2=0.0, op0=ALU.mult, op1=ALU.add),
             reads=[("c2", l)], writes=[("c1", l)])

    total_pieces = NT * L * NPIECE
    wstate = {"emitted": 0, "conv": 0}

    def ensure_conv(upto):
        upto = min(upto, L * NPIECE - 1)
        while wstate["conv"] <= upto:
            gi = wstate["conv"]
            l, p = divmod(gi, NPIECE)
            _, _, off, n = PIECES[p]
            S.dma("pool", f"cv{gi % 4}",
                  lambda e, l=l, off=off, n=n: e.dma_start(out=wbf[l, :, off:off + n], in_=wts[l, :, off:off + n]),
                  writes=[("wbf", l, p)])
            wstate["conv"] += 1

    def prefetch(upto):
        upto = min(upto, total_pieces - 1)
        while wstate["emitted"] <= upto:
            gi = wstate["emitted"]
            tl, p = divmod(gi, NPIECE)
            l = tl % L
            _, _, off, n = PIECES[p]
            if gi < L * NPIECE:
                ensure_conv(gi + 4)
            s = gi % NS
            if PIECES[p][0] == "g":
                S.dma("sp", "wg",
                      lambda e, l=l, off=off, n=n: e.dma_start(out=wgbuf[:, 0:n], in_=wbf[l, :, off:off + n]),
                      reads=[("wbf", l, p)], writes=["wgbuf"])
            else:
                S.dma("sp", f"w{s}",
                      lambda e, l=l, off=off, n=n, s=s: e.dma_start(out=ring[s][:, 0:n], in_=wbf[l, :, off:off + n]),
                      reads=[("wbf", l, p)], writes=[("ring", s)])
            wstate["emitted"] += 1

    def wpiece(ti, l, p):
        gi = (ti * L + l) * NPIECE + p
        prefetch(gi + NS - 1)
        s = gi % NS
        if PIECES[p][0] == "g":
            return wgbuf, "wgbuf"
        return ring[s], ("ring", s)

    def mm_group(bank, items, reads, n=T):
        def fn(e, items=items, bank=bank, n=n):
            ins = None
            last = len(items) - 1
            for i, (lt, rh) in enumerate(items):
                ins = e.matmul(ps[bank][:, 0:n], lt, rh, start=(i == 0), stop=(i == last))
            return ins
        return S.op("pe", fn, reads=reads, writes=[("ps", bank)])

    def rms_stats(src_key_list, nchunks, ones_t, ones_key, rbuf, rkey):
        b = next_bank()
        mm_group(b, [(ones_t[:], sq[:, c * T:(c + 1) * T]) for c in range(nchunks)],
                 reads=[ones_key] + [("sq", c) for c in range(nchunks)])
        S.op("act", lambda e, b=b: e.activation(out=rbuf[:], in_=ps[b][:, 0:T], func=AF.Ln, bias=eps_c, scale=1.0),
             reads=["cst"], writes=[("ps", b), rkey])
        S.op("act", lambda e: e.activation(out=rbuf[:], in_=rbuf[:], func=AF.Exp, scale=-0.5), writes=[rkey])

    def pre_norm(l, goff, rb, hc, hk):
        rbuf, rkey = rstd[rb], ("rstd", rb)
        S.op("act", lambda e: e.activation(out=sq[:], in_=hc[:], func=AF.Square),
             reads=[hk(c) for c in range(NCH)], writes=[("sq", c) for c in range(NCH)])
        rms_stats(None, NCH, ones_bf, "ones_bf", rbuf, rkey)
        for c in range(NCH):
            S.op("dve", lambda e, c=c: e.scalar_tensor_tensor(
                out=z[:, c * T:(c + 1) * T], in0=hc[:, c * T:(c + 1) * T], scalar=col(l, goff, c), in1=rbuf[:],
                op0=ALU.mult, op1=ALU.mult),
                reads=[hk(c), rkey, "prm"], writes=[("z", c)])

    XL_KEYS = [("xl", m) for m in range(NCH)] + ["xlh"]
    GG_KEYS = [("gg", m) for m in range(NCH)]
    CB_KEYS = [("cb", j) for j in range(4)] + ["cbh"]
    AF_KEYS = [("affn", j) for j in range(24)]
    out_toks = []

    def layer_body(ti, l):
        t0 = ti * T
        last_layer = (l == L - 1)
        hcur, hoth = ti % 2, (ti + 1) % 2
        hc = XY[hcur]
        Rb = XY[hoth][:, 0:4 * T]
        IGb = XY[hoth][:, 4 * T:8 * T]

        def hk(c):
            return ("xy", hcur, c)

        def rk(i):
            return ("xy", hoth, i)

        def igk(i):
            return ("xy", hoth, 4 + i)

        S.alias(XL_KEYS + GG_KEYS + CB_KEYS, AF_KEYS)
        pre_norm(l, P_GPRE, 0, hc, hk)
        S.op("pool", lambda e, l=l: e.tensor_copy(out=v3(xl, NCH)[:, :, 0:3],
                                                 in_=v3(tailx[:, l * 24:(l + 1) * 24], NCH)),
             reads=[("tailx", l)], writes=["xlh"])
        S.op("pool", lambda e, l=l: e.tensor_copy(out=v3(cbuf, 4)[:, :, 0:30],
                                                 in_=v3(tailc[:, l * 120:(l + 1) * 120], 4)),
             reads=[("tailc", l)], writes=["cbh"])
        sqf = sq[:].bitcast(F32)
        LBUF = [
            dict(R=Rb, Rk=lambda i: [rk(i)], IG=IGb, IGk=lambda i: [igk(i)], M=Mb, Mk=lambda i: [("M", i)]),
            dict(R=o[:, 0:4 * T], Rk=lambda i: [("o", i)], IG=sqf, IGk=lambda i: [("sq", 2 * i), ("sq", 2 * i + 1)],
                 M=o[:, 4 * T:8 * T], Mk=lambda i: [("o", 4 + i)]),
        ]

        def in_chunk(wt, wkey, i, oc):
            b = next_bank()
            mm_group(b, [(wt[:, i * 1024 + kc * 128: i * 1024 + (kc + 1) * 128], z[:, kc * T:(kc + 1) * T])
                         for kc in range(NCH)],
                     reads=[wkey] + [("z", kc) for kc in range(NCH)])
            if oc >= 20:
                j = oc - 20
                S.op("act", lambda e, b=b, j=j: e.activation(out=sig[j % 2][:], in_=ps[b][:, 0:T], func=AF.Sigmoid),
                     writes=[("ps", b), ("sig", j % 2)])
            elif oc >= 16:
                j = oc - 16
                S.op("dve", lambda e, b=b, j=j: e.tensor_tensor(
                    out=cbuf[:, j * CBW + 30:(j + 1) * CBW], in0=ps[b][:, 0:T], in1=sig[j % 2][:], op=ALU.mult),
                    reads=[("sig", j % 2)], writes=[("ps", b), ("cb", j)])
            elif oc < 8:
                m = oc
                S.op("act", lambda e, b=b, m=m: e.activation(out=xl[:, m * XLW + 3:(m + 1) * XLW], in_=ps[b][:, 0:T], func=AF.Copy),
                     writes=[("ps", b), ("xl", m)])
            else:
                m = oc - 8
                S.op("act", lambda e, b=b, m=m: e.activation(out=gg[:, m * T:(m + 1) * T], in_=ps[b][:, 0:T], func=AF.Gelu_apprx_tanh),
                     writes=[("ps", b), ("gg", m)])

        def build_dg4(bt):
            wc4 = prm[:, l * NP + P_LCW + bt * 16: l * NP + P_LCW + bt * 16 + 16]
            S.op("dve", lambda e, wc4=wc4: e.tensor_tensor(
                out=dg4[:].rearrange("p (k c) -> p k c", k=16),
                in0=ident_bf[:].unsqueeze(1).broadcast_to([128, 16, 128]),
                in1=wc4.unsqueeze(2).broadcast_to([128, 16, 128]), op=ALU.mult),
                reads=["ident", "prm"], writes=["dg4"])

        dgslot = {}
        dgc = [0]

        def build_dg31(j, half):
            g0 = half * 16
            nk = min(16, 31 - g0)
            slot = dgc[0] % 3
            dgc[0] += 1
            dgslot[(j, half)] = slot
            wc = prm[:, l * NP + P_CW + j * 31 + g0: l * NP + P_CW + j * 31 + g0 + nk]
            S.op("dve", lambda e, slot=slot, nk=nk, wc=wc: e.tensor_tensor(
                out=dg31[slot][:, 0:nk * 128].rearrange("p (k c) -> p k c", k=nk),
                in0=ident_bf[:].unsqueeze(1).broadcast_to([128, nk, 128]),
                in1=wc.unsqueeze(2).broadcast_to([128, nk, 128]), op=ALU.mult),
                reads=["ident", "prm"], writes=[("dg31", slot)])

        cbank = {}

        def conv31_mm(j, half):
            if half == 0:
                cbank[j] = next_bank()
            b = cbank[j]
            g0 = half * 16
            ks = list(range(g0, min(g0 + 16, 31)))
            slot = dgslot[(j, half)]

            def fn(e, ks=ks, slot=slot, b=b, j=j):
                ins = None
                for kk, k in enumerate(ks):
                    ins = e.matmul(ps[b][:, 0:T], dg31[slot][:, kk * 128:(kk + 1) * 128],
                                   cbuf[:, j * CBW + k: j * CBW + k + T], start=(k == 0), stop=(k == 30))
                return ins
            S.op("pe", fn, reads=[("dg31", slot), ("cb", j), "cbh"], writes=[("ps", b)])
            if half == 1:
                S.op("act", lambda e, b=b, j=j: e.activation(out=cc[:, j * T:(j + 1) * T], in_=ps[b][:, 0:T], func=AF.Identity,
                                                             bias=col(l, P_CB, j), scale=1.0),
                     reads=["prm"], writes=[("ps", b), ("cc", j)])

        def lru_front(bt, wg, wgkey):
            LB = LBUF[bt]
            for i in range(4):
                m = bt * 4 + i
                b = next_bank()
                mm_group(b, [(dg4[:, (i * 4 + k) * 128:(i * 4 + k + 1) * 128], xl[:, m * XLW + k: m * XLW + k + T]) for k in range(4)],
                         reads=["dg4", ("xl", m), "xlh"])
                S.op("act", lambda e, b=b, i=i, m=m: e.activation(out=xcb[:, i * T:(i + 1) * T], in_=ps[b][:, 0:T], func=AF.Identity,
                                                                 bias=col(l, P_LCB, m), scale=1.0),
                     reads=["prm"], writes=[("ps", b), ("xcb", i)])
                S.op("act", lambda e, b=b, i=i, m=m: e.activation(out=xcf[:, i * T:(i + 1) * T], in_=ps[b][:, 0:T], func=AF.Identity,
                                                                 bias=col(l, P_LCB, m), scale=1.0),
                     reads=["prm"], writes=[("ps", b), ("xcf", i)])
            for i in range(4):
                m = bt * 4 + i
                br = next_bank()
                mm_group(br, [(wg[:, m * 128:(m + 1) * 128], xcb[:, i * T:(i + 1) * T])], reads=[wgkey, ("xcb", i)])
                S.op("act", lambda e, br=br, i=i, m=m, LB=LB: e.activation(out=LB["R"][:, i * T:(i + 1) * T], in_=ps[br][:, 0:T],
                                                                         func=AF.Sigmoid, bias=col(l, P_BA, m), scale=1.0),
                     reads=["prm"], writes=[("ps", br)] + LB["Rk"](i))
                bi = next_bank()
                mm_group(bi, [(wg[:, 1024 + m * 128:1024 + (m + 1) * 128], xcb[:, i * T:(i + 1) * T])], reads=[wgkey, ("xcb", i)])
                S.op("act", lambda e, bi=bi, i=i, m=m, LB=LB: e.activation(out=LB["IG"][:, i * T:(i + 1) * T], in_=ps[bi][:, 0:T],
                                                                         func=AF.Sigmoid, bias=col(l, P_BX, m), scale=1.0),
                     reads=["prm"], writes=[("ps", bi)] + LB["IGk"](i))

        def lru_act_tail(bt):
            LB = LBUF[bt]
            for i in range(4):
                m = bt * 4 + i
                S.op("act", lambda e, i=i, m=m, LB=LB: e.activation(out=LB["M"][:, i * T:(i + 1) * T], in_=LB["R"][:, i * T:(i + 1) * T],
                                                                   func=AF.Exp, scale=cst[:, C2 + l * NCH + m:C2 + l * NCH + m + 1]),
                     reads=LB["Rk"](i) + [("c2", l)], writes=LB["Mk"](i))
            for i in range(4):
                m = bt * 4 + i
                S.op("act", lambda e, i=i, m=m, LB=LB: e.activation(out=LB["R"][:, i * T:(i + 1) * T], in_=LB["R"][:, i * T:(i + 1) * T],
                                                                   func=AF.Exp, scale=cst[:, C1 + l * NCH + m:C1 + l * NCH + m + 1]),
                     reads=[("c1", l)], writes=LB["Rk"](i))
            for i in range(4):
                S.op("act", lambda e, i=i, LB=LB: e.activation(out=LB["M"][:, i * T:(i + 1) * T], in_=LB["M"][:, i * T:(i + 1) * T],
                                                              func=AF.Sqrt, bias=one_c, scale=-1.0),
                     reads=["cst1"], writes=LB["Mk"](i))

        def lru_dve_a(bt):
            LB = LBUF[bt]
            for i in range(4):
                S.op("dve", lambda e, i=i, LB=LB: e.tensor_tensor(out=LB["IG"][:, i * T:(i + 1) * T], in0=LB["IG"][:, i * T:(i + 1) * T],
                                                                 in1=xcf[:, i * T:(i + 1) * T], op=ALU.mult),
                     reads=[("xcf", i)], writes=LB["IGk"](i))
            for i in range(4):
                S.op("dve", lambda e, i=i, LB=LB: e.tensor_tensor(out=LB["M"][:, i * T:(i + 1) * T], in0=LB["M"][:, i * T:(i + 1) * T],
                                                                 in1=LB["IG"][:, i * T:(i + 1) * T], op=ALU.mult),
                     reads=LB["IGk"](i), writes=LB["Mk"](i))

        def lru_dve_b(bt):
            LB = LBUF[bt]
            for i in range(4):
                m = bt * 4 + i
                hb = i % 2
                S.op("dve", lambda e, i=i, m=m, hb=hb, LB=LB: e.tensor_tensor_scan(
                    out=hl[hb][:], data0=LB["R"][:, i * T:(i + 1) * T], data1=LB["M"][:, i * T:(i + 1) * T],
                    initial=state[:, l * NCH + m:l * NCH + m + 1], op0=ALU.mult, op1=ALU.add),
                    reads=LB["Rk"](i) + LB["Mk"](i) + [("st", l, m)], writes=[("hl", hb)])
                S.op("pool", lambda e, m=m, hb=hb: e.tensor_copy(out=state[:, l * NCH + m:l * NCH + m + 1], in_=hl[hb][:, T - 1:T]),
                     reads=[("hl", hb)], writes=[("st", l, m)])
                S.op("dve", lambda e, m=m, hb=hb: e.tensor_tensor(out=y[:, m * T:(m + 1) * T], in0=hl[hb][:],
                                                                 in1=gg[:, m * T:(m + 1) * T], op=ALU.mult),
                     reads=[("hl", hb), ("gg", m)], writes=[("y", m)])

        for q in range(2):
            wt, wkey = wpiece(ti, l, q)
            for i in range(4):
                in_chunk(wt, wkey, i, IN_ORDER[q * 4 + i])
        S.op("pool", lambda e, l=l: e.tensor_copy(out=v3(tailx[:, l * 24:(l + 1) * 24], NCH),
                                                 in_=v3(xl, NCH)[:, :, T:T + 3]),
             reads=[("xl", m) for m in range(NCH)], writes=[("tailx", l)])
        wg, wgkey = wpiece(ti, l, 2)
        build_dg4(0)
        build_dg31(0, 0)
        build_dg31(0, 1)
        build_dg31(1, 0)
        lru_front(0, wg, wgkey)
        lru_act_tail(0)
        build_dg4(1)
        for q in range(2):
            wt, wkey = wpiece(ti, l, 3 + q)
            for i in range(4):
                in_chunk(wt, wkey, i, IN_ORDER[16 + q * 4 + i])
        lru_dve_a(0)
        lru_front(1, wg, wgkey)
        lru_dve_b(0)
        lru_act_tail(1)
        for q in range(2):
            wt, wkey = wpiece(ti, l, 5 + q)
            for i in range(4):
                in_chunk(wt, wkey, i, IN_ORDER[8 + q * 4 + i])
        S.op("pool", lambda e, l=l: e.tensor_copy(out=v3(tailc[:, l * 120:(l + 1) * 120], 4),
                                                 in_=v3(cbuf, 4)[:, :, T:T + 30]),
             reads=[("cb", j) for j in range(4)], writes=[("tailc", l)])
        conv31_mm(0, 0)
        build_dg31(1, 1)
        conv31_mm(0, 1)
        conv31_mm(1, 0)
        build_dg31(2, 0)
        lru_dve_a(1)
        conv31_mm(1, 1)
        build_dg31(2, 1)
        conv31_mm(2, 0)
        build_dg31(3, 0)
        lru_dve_b(1)
        conv31_mm(2, 1)
        build_dg31(3, 1)
        conv31_mm(3, 0)
        conv31_mm(3, 1)
        S.op("act", lambda e: e.activation(out=sq[:, 0:8 * T], in_=y[:, 0:8 * T], func=AF.Square),
             reads=[("y", m) for m in range(8)], writes=[("sq", c) for c in range(8)])
        rms_stats(None, 8, ones_bf, "ones_bf", rstd[0], ("rstd", 0))
        for m in range(8):
            S.op("dve", lambda e, m=m: e.scalar_tensor_tensor(
                out=y[:, m * T:(m + 1) * T], in0=y[:, m * T:(m + 1) * T], scalar=col(l, P_GOL, m), in1=rstd[0][:],
                op0=ALU.mult, op1=ALU.mult), reads=[("rstd", 0), "prm"], writes=[("y", m)])
        bd = []
        for j in range(4):
            bj = next_bank()
            bd.append(bj)
            mm_group(bj, [(cen_f[:], cc[:, j * T:(j + 1) * T])], reads=["cen_f", ("cc", j)])
        for j in range(4):
            S.op("act", lambda e, j=j, bj=bd[j]: e.activation(out=IGb[:, j * T:(j + 1) * T], in_=ps[bj][:, 0:T], func=AF.Square),
                 writes=[("ps", bd[j]), igk(j)])
        bv = []
        for j in range(4):
            bj = next_bank()
            bv.append(bj)
            mm_group(bj, [(ones_f[:], IGb[:, j * T:(j + 1) * T])], reads=["ones_f", igk(j)])
        for j in range(4):
            S.op("act", lambda e, j=j, bj=bv[j]: e.activation(out=Rb[:, j * T:(j + 1) * T], in_=ps[bj][:, 0:T], func=AF.Ln,
                                                            bias=eps_c, scale=1.0),
                 reads=["cst"], writes=[("ps", bv[j]), rk(j)])
        for j in range(4):
            S.op("act", lambda e, j=j: e.activation(out=Rb[:, j * T:(j + 1) * T], in_=Rb[:, j * T:(j + 1) * T], func=AF.Exp, scale=-0.5),
                 writes=[rk(j)])
        for j in range(4):
            S.op("dve", lambda e, j=j, bj=bd[j]: e.tensor_tensor(out=IGb[:, j * T:(j + 1) * T], in0=ps[bj][:, 0:T],
                                                               in1=Rb[:, j * T:(j + 1) * T], op=ALU.mult),
                 reads=[rk(j)], writes=[("ps", bd[j]), igk(j)])
        for j in range(4):
            S.op("act", lambda e, j=j: e.activation(out=y[:, (8 + j) * T:(9 + j) * T], in_=IGb[:, j * T:(j + 1) * T], func=AF.Silu,
                                                   scale=col(l, P_LNG, j), bias=col(l, P_LNB, j)),
                 reads=[igk(j), "prm"], writes=[("y", 8 + j)])
        S.op("act", lambda e: e.activation(out=sq[:, 0:4 * T], in_=y[:, 8 * T:12 * T], func=AF.Square),
             reads=[("y", 8 + j) for j in range(4)], writes=[("sq", c) for c in range(4)])
        rms_stats(None, 4, ones5_bf, "ones5_bf", rstd[1], ("rstd", 1))
        for j in range(4):
            S.op("dve", lambda e, j=j: e.scalar_tensor_tensor(
                out=y[:, (8 + j) * T:(9 + j) * T], in0=y[:, (8 + j) * T:(9 + j) * T], scalar=col(l, P_GOC, j), in1=rstd[1][:],
                op0=ALU.mult, op1=ALU.mult), reads=[("rstd", 1), "prm"], writes=[("y", 8 + j)])
        for q in range(4):
            wt, wkey = wpiece(ti, l, 7 + q)
            for i in range(2):
                m = q * 2 + i
                b = next_bank()
                mm_group(b, [(wt[:, i * 1536 + kc * 128: i * 1536 + (kc + 1) * 128], y[:, kc * T:(kc + 1) * T])
                             for kc in range(12)],
                         reads=[wkey] + [("y", kc) for kc in range(12)])
                S.op("act", lambda e, b=b, m=m: e.activation(out=o[:, m * T:(m + 1) * T], in_=ps[b][:, 0:T], func=AF.Copy),
                     writes=[("ps", b), ("o", m)])
                S.op("act", lambda e, b=b, m=m: e.activation(out=sq[:, m * T:(m + 1) * T], in_=ps[b][:, 0:T], func=AF.Square),
                     writes=[("ps", b), ("sq", m)])
        rms_stats(None, NCH, ones_bf, "ones_bf", rstd[0], ("rstd", 0))
        for m in range(NCH):
            S.op("dve", lambda e, m=m: e.scalar_tensor_tensor(
                out=o[:, m * T:(m + 1) * T], in0=o[:, m * T:(m + 1) * T], scalar=col(l, P_GPM, m), in1=rstd[0][:],
                op0=ALU.mult, op1=ALU.mult), reads=[("rstd", 0), "prm"], writes=[("o", m)])
        for m in range(NCH):
            S.op("dve", lambda e, m=m: e.tensor_tensor(out=hc[:, m * T:(m + 1) * T], in0=hc[:, m * T:(m + 1) * T],
                                                      in1=o[:, m * T:(m + 1) * T], op=ALU.add),
                 reads=[("o", m)], writes=[hk(m)])
        S.alias(AF_KEYS, XL_KEYS + GG_KEYS + CB_KEYS)
        if last_layer and ti + 1 < NT:
            t1 = (ti + 1) * T
            S.dma("act", "hld",
                  lambda e, t1=t1, hoth=hoth: e.dma_start(out=v3(XY[hoth][:], NCH), in_=hT[:, :, t1:t1 + T].rearrange("c p t -> p c t")),
                  writes=[("xy", hoth, c) for c in range(NCH)])
        pre_norm(l, P_GPF, 1, hc, hk)
        def ffn_a(j):
            q, i = divmod(j, 2)
            wt, wkey = wpiece(ti, l, 11 + q)
            s = j % 2
            yb = ybuf[s]
            S.op("pool", lambda e, yb=yb, j=j: e.tensor_copy(
                out=v3(yb[:], 2)[:, :, 0:2], in_=v3(taily[:, (l * 24 + j) * 4:(l * 24 + j + 1) * 4], 2)),
                reads=[("taily", l, j)], writes=[("ybh", s)])
            tb = []
            for gu in range(2):
                b = next_bank()
                base = i * 2048 + gu * 1024
                mm_group(b, [(wt[:, base + kc * 128: base + (kc + 1) * 128], z[:, kc * T:(kc + 1) * T])
                             for kc in range(NCH)],
                         reads=[wkey] + [("z", kc) for kc in range(NCH)])
                idx = j * 2 + gu
                tt = tbuf[s * 2 + gu]
                tk = ("tb", s * 2 + gu)
                S.op("act", lambda e, b=b, yb=yb, gu=gu: e.activation(out=yb[:, gu * YBW + 2:(gu + 1) * YBW], in_=ps[b][:, 0:T], func=AF.Copy),
                     writes=[("ps", b), ("yb", s, gu)])
                S.op("act", lambda e, b=b, tt=tt, idx=idx: e.activation(out=tt[:], in_=ps[b][:, 0:T], func=AF.Identity,
                                                                        scale=col(l, P_FCW, 2 * 48 + idx), bias=col(l, P_FCB, idx)),
                     reads=["prm"], writes=[("ps", b), tk])
                tb.append((tt, tk, idx))
            S.op("pool", lambda e, yb=yb, j=j: e.tensor_copy(
                out=v3(taily[:, (l * 24 + j) * 4:(l * 24 + j + 1) * 4], 2), in_=v3(yb[:], 2)[:, :, T:T + 2]),
                reads=[("yb", s, 0), ("yb", s, 1)], writes=[("taily", l, j)])
            for k in (1, 0):
                for gu in range(2):
                    tt, tk, idx = tb[gu]
                    S.op("dve", lambda e, yb=yb, gu=gu, tt=tt, idx=idx, k=k: e.scalar_tensor_tensor(
                        out=tt[:], in0=yb[:, gu * YBW + k: gu * YBW + k + T], scalar=col(l, P_FCW, k * 48 + idx), in1=tt[:],
                        op0=ALU.mult, op1=ALU.add),
                        reads=[("yb", s, gu), ("ybh", s), "prm"], writes=[tk])
            return tb

        def ffn_b(j, tb):
            s = j % 2
            S.op("act", lambda e, s=s, tt=tb[0][0]: e.activation(out=gl[s][:], in_=tt[:], func=AF.Gelu_apprx_tanh),
                 reads=[tb[0][1]], writes=[("gl", s)])
            S.op("dve", lambda e, s=s, j=j, tt=tb[1][0]: e.tensor_tensor(out=affn[:, j * T:(j + 1) * T], in0=gl[s][:], in1=tt[:], op=ALU.mult),
                 reads=[("gl", s), tb[1][1]], writes=[("affn", j)])

        if FFN_PIPE:
            prev = ffn_a(0)
            for j in range(1, 24):
                cur = ffn_a(j)
                ffn_b(j - 1, prev)
                prev = cur
            ffn_b(23, prev)
        else:
            for j in range(24):
                ffn_b(j, ffn_a(j))
        for m in range(NCH):
            wt, wkey = wpiece(ti, l, 23 + m)
            b = next_bank()
            mm_group(b, [(wt[:, kc * 128:(kc + 1) * 128], affn[:, kc * T:(kc + 1) * T]) for kc in range(24)],
                     reads=[wkey] + AF_KEYS)
            S.op("act", lambda e, b=b, m=m: e.activation(out=o[:, m * T:(m + 1) * T], in_=ps[b][:, 0:T], func=AF.Copy),
                 writes=[("ps", b), ("o", m)])
            S.op("act", lambda e, b=b, m=m: e.activation(out=sq[:, m * T:(m + 1) * T], in_=ps[b][:, 0:T], func=AF.Square),
                 writes=[("ps", b), ("sq", m)])
        rms_stats(None, NCH, ones_bf, "ones_bf", rstd[0], ("rstd", 0))
        for m in range(NCH):
            S.op("dve", lambda e, m=m: e.scalar_tensor_tensor(
                out=o[:, m * T:(m + 1) * T], in0=o[:, m * T:(m + 1) * T], scalar=col(l, P_GPO, m), in1=rstd[0][:],
                op0=ALU.mult, op1=ALU.mult), reads=[("rstd", 0), "prm"], writes=[("o", m)])
        for m in range(NCH):
            if last_layer:
                S.op("dve", lambda e, m=m: e.tensor_tensor(out=o[:, m * T:(m + 1) * T], in0=o[:, m * T:(m + 1) * T],
                                                          in1=hc[:, m * T:(m + 1) * T], op=ALU.add),
                     reads=[hk(m)], writes=[("o", m)])
            else:
                S.op("dve", lambda e, m=m: e.tensor_tensor(out=hc[:, m * T:(m + 1) * T], in0=hc[:, m * T:(m + 1) * T],
                                                          in1=o[:, m * T:(m + 1) * T], op=ALU.add),
                     reads=[("o", m)], writes=[hk(m)])

    for ti in range(NT):
        t0 = ti * T
        if ti == 0:
            S.dma("act", "hld",
                  lambda e: e.dma_start(out=v3(XY[0][:], NCH), in_=hT[:, :, 0:T].rearrange("c p t -> p c t")),
                  writes=[("xy", 0, c) for c in range(NCH)])
        for l in range(L):
            layer_body(ti, l)
        a = NMETA if ti == 0 else 0
        d0 = t0 + a - NMETA
        n = T - a
        out_toks.append(S.dma(
            "act", "st",
            lambda e, a=a, d0=d0, n=n: e.dma_start(out=oT[:, :, d0:d0 + n].rearrange("c p t -> p c t"),
                                                   in_=v3(o[:], NCH)[:, :, a:T]),
            reads=[("o", m) for m in range(NCH)]))
    S.final_wait("act", out_toks)
    if emit:
        S.emit()
    return nc, S


def _cols(v, nch):
    return np.ascontiguousarray(np.asarray(v, np.float32).reshape(nch, 128).T)


def pack_weights(w_in, lru_wa, lru_wx, w_out, w_up, w_down, L):
    out = np.empty((L, 128, WCOLS), np.float32)
    for l in range(L):
        a = np.asarray(w_in[l], np.float32).reshape(8, 128, 24, 128)[:, :, IN_ORDER, :]
        out[l, :, OFF_IN:OFF_G] = a.transpose(1, 2, 0, 3).reshape(128, -1)
        out[l, :, OFF_G:OFF_G + 1024] = np.asarray(lru_wa[l], np.float32).transpose(1, 0, 2).reshape(128, -1)
        out[l, :, OFF_G + 1024:OFF_OUT] = np.asarray(lru_wx[l], np.float32).transpose(1, 0, 2).reshape(128, -1)
        a = np.asarray(w_out[l], np.float32).reshape(12, 128, 8, 128)
        out[l, :, OFF_OUT:OFF_UP] = a.transpose(1, 2, 0, 3).reshape(128, -1)
        a = np.asarray(w_up[l], np.float32).reshape(8, 128, 2, 24, 128)
        out[l, :, OFF_UP:OFF_DOWN] = a.transpose(1, 3, 2, 0, 4).reshape(128, -1)
        a = np.asarray(w_down[l], np.float32).reshape(24, 128, 8, 128)
        out[l, :, OFF_DOWN:WCOLS] = a.transpose(1, 2, 0, 3).reshape(128, -1)
    return out


def pack_params(inp, L):
    prm = np.zeros((128, L * NP), np.float32)
    for l in range(L):
        b = l * NP
        prm[:, b + P_GPRE:b + P_GPRE + 8] = _cols(inp["g_pre_mix"][l], 8)
        a = np.asarray(inp["lru_conv_w"][l], np.float32).reshape(4, 8, 128)
        prm[:, b + P_LCW:b + P_LCW + 32] = a.transpose(2, 1, 0).reshape(128, 32)
        prm[:, b + P_LCB:b + P_LCB + 8] = _cols(inp["lru_conv_b"][l], 8)
        prm[:, b + P_BA:b + P_BA + 8] = _cols(inp["lru_ba"][l], 8)
        prm[:, b + P_BX:b + P_BX + 8] = _cols(inp["lru_bx"][l], 8)
        prm[:, b + P_LAM:b + P_LAM + 8] = _cols(inp["lru_lambda"][l], 8)
        a = np.asarray(inp["conv_w"][l], np.float32).reshape(31, 4, 128)
        prm[:, b + P_CW:b + P_CW + 124] = a.transpose(2, 1, 0).reshape(128, 124)
        prm[:, b + P_CB:b + P_CB + 4] = _cols(inp["conv_b"][l], 4)
        prm[:, b + P_LNG:b + P_LNG + 4] = _cols(inp["conv_ln_g"][l], 4)
        prm[:, b + P_LNB:b + P_LNB + 4] = _cols(inp["conv_ln_b"][l], 4)
        prm[:, b + P_GOL:b + P_GOL + 8] = _cols(inp["g_out_lru"][l], 8)
        prm[:, b + P_GOC:b + P_GOC + 4] = _cols(inp["g_out_conv"][l], 4)
        prm[:, b + P_GPM:b + P_GPM + 8] = _cols(inp["g_post_mix"][l], 8)
        prm[:, b + P_GPF:b + P_GPF + 8] = _cols(inp["g_pre_ffn"][l], 8)
        for k in range(3):
            a = np.asarray(inp["ffn_conv_w"][l][k], np.float32).reshape(2, 24, 128)
            prm[:, b + P_FCW + k * 48:b + P_FCW + (k + 1) * 48] = a.transpose(2, 1, 0).reshape(128, 48)
        a = np.asarray(inp["ffn_conv_b"][l], np.float32).reshape(2, 24, 128)
        prm[:, b + P_FCB:b + P_FCB + 48] = a.transpose(2, 1, 0).reshape(128, 48)
        prm[:, b + P_GPO:b + P_GPO + 8] = _cols(inp["g_post_ffn"][l], 8)
    return prm


_CACHE = {}


def run_cores(x, meta_tokens, inp, NT, L, n_cores):
    B, S_, _ = x.shape
    key = (NT, L)
    if key not in _CACHE:
        _CACHE[key] = build_nc(NT, L)[0]
    nc = _CACHE[key]
    wts = pack_weights(inp["w_in"], inp["lru_wa"], inp["lru_wx"], inp["w_out"], inp["w_up"], inp["w_down"], L)
    prm = pack_params(inp, L)
    meta = np.asarray(meta_tokens, np.float32)
    in_maps = []
    for b in range(n_cores):
        hfull = np.concatenate([meta, np.asarray(x[b], np.float32)], axis=0)
        hT = np.ascontiguousarray(hfull.T).reshape(NCH, 128, NT * T)
        in_maps.append({"hT": hT, "wts": wts, "prm": prm, "ident": np.eye(128, dtype=np.float32),
                        "cen": (np.eye(128) - 1.0 / 128.0).astype(np.float32)})
    res = run_bass_kernel_spmd(nc, in_maps, core_ids=list(range(n_cores)))
    outs = []
    for b in range(n_cores):
        oT = np.asarray(res.results[b]["oT"]).reshape(D, NT * T - NMETA)
        outs.append(np.ascontiguousarray(oT.T))
    return np.stack(outs, axis=0)


def kernel(**inputs):
    x = np.asarray(inputs["x"], np.float32)
    out = run_cores(x, inputs["meta_tokens"], inputs, NT_FULL, DEPTH, 8)
    return out.astype(np.float32)
```
